# Optimizing a Trainium2 kernel written in Bass

```python
import jax, jax.numpy as jnp
from jax import lax
import numpy as np

D_MODEL = 2048
BATCH = 8
SEQ = 2048
DEPTH = 1

CTX_LEN = 256
GRID_W = 64
Q_BLOCK = 128
ROPE_THETA = 10000.0
NORM_EPS = 1e-6

MLA_HEADS = 8
MLA_Q_LORA = 768
MLA_KV_LORA = 512
MLA_NOPE = 128
MLA_ROPE = 64
MLA_V = 128
GQA_HEADS = 8
GQA_KV_HEADS = 2
GQA_HEAD_DIM = 128
D_FF = 5632
CONV_W = 3
N_BRANCH = 2

KV_COLS = MLA_KV_LORA + MLA_ROPE + 2 * GQA_KV_HEADS * GQA_HEAD_DIM
Q_COLS = MLA_Q_LORA + GQA_HEADS * GQA_HEAD_DIM
GATE_COLS = N_BRANCH * D_MODEL
IN_COLS = KV_COLS + Q_COLS + GATE_COLS
KV_SPLITS = [MLA_KV_LORA, MLA_KV_LORA + MLA_ROPE, MLA_KV_LORA + MLA_ROPE + GQA_KV_HEADS * GQA_HEAD_DIM]

kernel_name = "hybrid_mla_gqa_convffn_dit_prefix"


def rms_norm(x, g):
    xf = x.astype(jnp.float32)
    y = xf * lax.rsqrt(jnp.mean(xf * xf, axis=-1, keepdims=True) + NORM_EPS)
    return (y * g.astype(jnp.float32)).astype(x.dtype)


def modulate(h, shift, scale):
    return h * (1 + scale) + shift


def ada_terms(cond, w_ada, b_ada):
    return jnp.split(jax.nn.silu(cond) @ w_ada + b_ada, 6, axis=-1)


def grid_rope_tables(n_rows, rot_dim):
    row = jnp.repeat(jnp.arange(n_rows, dtype=jnp.float32), GRID_W)
    col = jnp.tile(jnp.arange(GRID_W, dtype=jnp.float32), n_rows)
    half = rot_dim // 2
    inv_freq = ROPE_THETA ** (-jnp.arange(0, half, 2, dtype=jnp.float32) / half)
    ang = jnp.concatenate([row[:, None] * inv_freq, col[:, None] * inv_freq], axis=-1)
    return jnp.cos(ang), jnp.sin(ang)


def apply_grid_rope(x, cos, sin):
    b, t, h, r = x.shape
    q = r // 4
    xs = x.reshape(b, t, h, 2, 2, q)
    x1, x2 = xs[..., 0, :], xs[..., 1, :]
    c = cos.reshape(t, 1, 2, q).astype(x.dtype)
    s = sin.reshape(t, 1, 2, q).astype(x.dtype)
    out = jnp.stack([x1 * c - x2 * s, x1 * s + x2 * c], axis=-2)
    return out.reshape(b, t, h, r)


def block_attention(q, k, v):
    b, tq, hk, g, dk = q.shape
    dv = v.shape[-1]
    scale = dk ** -0.5
    kf = k.astype(jnp.float32)
    qb = jnp.moveaxis(q.reshape(b, tq // Q_BLOCK, Q_BLOCK, hk, g, dk), 1, 0)

    def one_block(q_blk):
        s = jnp.einsum("bqhgd,bkhd->bhgqk", q_blk.astype(jnp.float32), kf) * scale
        p = jax.nn.softmax(s, axis=-1)
        return jnp.einsum("bhgqk,bkhd->bqhgd", p.astype(v.dtype), v)

    o = lax.map(one_block, qb)
    return jnp.moveaxis(o, 0, 1).reshape(b, tq, hk * g * dv)


def mixer_keys(kv, p, rope):
    b, t, _ = kv.shape
    c_kv, k_pe, k_b, v_b = jnp.split(kv, KV_SPLITS, axis=-1)
    kv_up = (rms_norm(c_kv, p["mla_kv_norm_g"]) @ p["w_kv_up"]).reshape(b, t, MLA_HEADS, MLA_NOPE + MLA_V)
    k_nope, v_a = jnp.split(kv_up, [MLA_NOPE], axis=-1)
    k_pe = k_pe.reshape(b, t, 1, MLA_ROPE)
    k_b = rms_norm(k_b.reshape(b, t, GQA_KV_HEADS, GQA_HEAD_DIM), p["gqa_k_norm_g"])
    v_b = v_b.reshape(b, t, GQA_KV_HEADS, GQA_HEAD_DIM)
    if rope is not None:
        cos_a, sin_a, cos_b, sin_b = rope
        k_pe = apply_grid_rope(k_pe, cos_a, sin_a)
        k_b = apply_grid_rope(k_b, cos_b, sin_b)
    k_a = jnp.concatenate([k_nope, jnp.broadcast_to(k_pe, (b, t, MLA_HEADS, MLA_ROPE))], axis=-1)
    return (k_a, v_a, k_b, v_b)


def mixer_queries(qp, p, rope):
    b, t, _ = qp.shape
    c_q, q_b = jnp.split(qp, [MLA_Q_LORA], axis=-1)
    q_a = (rms_norm(c_q, p["mla_q_norm_g"]) @ p["w_q_up"]).reshape(b, t, MLA_HEADS, MLA_NOPE + MLA_ROPE)
    q_nope, q_pe = jnp.split(q_a, [MLA_NOPE], axis=-1)
    q_b = rms_norm(q_b.reshape(b, t, GQA_HEADS, GQA_HEAD_DIM), p["gqa_q_norm_g"])
    if rope is not None:
        cos_a, sin_a, cos_b, sin_b = rope
        q_pe = apply_grid_rope(q_pe, cos_a, sin_a)
        q_b = apply_grid_rope(q_b, cos_b, sin_b)
    q_a = jnp.concatenate([q_nope, q_pe], axis=-1)[:, :, :, None, :]
    q_b = q_b.reshape(b, t, GQA_KV_HEADS, GQA_HEADS // GQA_KV_HEADS, GQA_HEAD_DIM)
    return q_a, q_b


def attend_and_merge(proj, keys, p, rope):
    q_a, q_b = mixer_queries(proj[..., KV_COLS:KV_COLS + Q_COLS], p, rope)
    g_a, g_b = jnp.split(jax.nn.sigmoid(proj[..., KV_COLS + Q_COLS:]), N_BRANCH, axis=-1)
    k_a, v_a, k_b, v_b = keys
    o_a = block_attention(q_a, k_a, v_a)
    o_b = block_attention(q_b, k_b, v_b)
    merged = g_a * (o_a @ p["w_br_a"]) + g_b * (o_b @ p["w_br_b"])
    return merged @ p["w_out"]


def conv_ffn(z, p):
    t = z.shape[1]
    u = z @ p["w_up"]
    pad = CONV_W // 2
    up = jnp.pad(u, ((0, 0), (pad, pad), (0, 0)))
    uc = p["conv_b"] + sum(p["conv_w"][j] * up[:, j:j + t] for j in range(CONV_W))
    a, bb = jnp.split(uc, 2, axis=-1)
    return (jax.nn.silu(a) * bb) @ p["w_down"]


def setup_inputs(seed: int = 0) -> dict:
    key = jax.random.key(seed)
    ks = jax.random.split(key, 24)
    f32 = jnp.float32

    def nrm(k, shape, scale):
        return jax.random.normal(k, shape, f32) * scale

    def gain(k, shape):
        return 1.0 + 0.01 * jax.random.normal(k, shape, f32)

    L, D = DEPTH, D_MODEL
    return {
        "x": nrm(ks[0], (BATCH, SEQ, D), 1.0),
        "c": nrm(ks[1], (BATCH, D), 1.0),
        "ctx": nrm(ks[2], (BATCH, CTX_LEN, D), 1.0),
        "c_ctx": nrm(ks[3], (D,), 0.5),
        "w_ada": nrm(ks[4], (L, D, 6 * D), 0.5 * D ** -0.5),
        "b_ada": nrm(ks[5], (L, 6 * D), 0.01),
        "norm1_g": gain(ks[6], (L, D)),
        "w_in": nrm(ks[7], (L, D, IN_COLS), D ** -0.5),
        "mla_q_norm_g": gain(ks[8], (L, MLA_Q_LORA)),
        "w_q_up": nrm(ks[9], (L, MLA_Q_LORA, MLA_HEADS * (MLA_NOPE + MLA_ROPE)), MLA_Q_LORA ** -0.5),
        "mla_kv_norm_g": gain(ks[10], (L, MLA_KV_LORA)),
        "w_kv_up": nrm(ks[11], (L, MLA_KV_LORA, MLA_HEADS * (MLA_NOPE + MLA_V)), MLA_KV_LORA ** -0.5),
        "gqa_q_norm_g": gain(ks[12], (L, GQA_HEAD_DIM)),
        "gqa_k_norm_g": gain(ks[13], (L, GQA_HEAD_DIM)),
        "w_br_a": nrm(ks[14], (L, MLA_HEADS * MLA_V, D), (MLA_HEADS * MLA_V) ** -0.5),
        "w_br_b": nrm(ks[15], (L, GQA_HEADS * GQA_HEAD_DIM, D), (GQA_HEADS * GQA_HEAD_DIM) ** -0.5),
        "w_out": nrm(ks[16], (L, D, D), D ** -0.5),
        "norm2_g": gain(ks[17], (L, D)),
        "w_up": nrm(ks[18], (L, D, 2 * D_FF), D ** -0.5),
        "conv_w": nrm(ks[19], (L, CONV_W, 2 * D_FF), CONV_W ** -0.5),
        "conv_b": nrm(ks[20], (L, 2 * D_FF), 0.01),
        "w_down": nrm(ks[21], (L, D_FF, D), D_FF ** -0.5),
        "final_norm_g": gain(ks[22], (D,)),
    }


def reference(x, c, ctx, c_ctx, w_ada, b_ada, norm1_g, w_in, mla_q_norm_g, w_q_up, mla_kv_norm_g,
              w_kv_up, gqa_q_norm_g, gqa_k_norm_g, w_br_a, w_br_b, w_out, norm2_g, w_up, conv_w,
              conv_b, w_down, final_norm_g):
    n_lat = x.shape[1]
    ROWS = n_lat // GRID_W
    rope = (*grid_rope_tables(ROWS, MLA_ROPE), *grid_rope_tables(ROWS, GQA_HEAD_DIM))
    cond_lat = c[:, None, :]
    cond_ctx = c_ctx[None, None, :]

    for l in range(DEPTH):
        p = {
            "w_in": w_in[l], "mla_q_norm_g": mla_q_norm_g[l], "w_q_up": w_q_up[l],
            "mla_kv_norm_g": mla_kv_norm_g[l], "w_kv_up": w_kv_up[l],
            "gqa_q_norm_g": gqa_q_norm_g[l], "gqa_k_norm_g": gqa_k_norm_g[l],
            "w_br_a": w_br_a[l], "w_br_b": w_br_b[l], "w_out": w_out[l],
            "w_up": w_up[l], "conv_w": conv_w[l], "conv_b": conv_b[l], "w_down": w_down[l],
        }
        last = l == DEPTH - 1
        sh1, sc1, g1, sh2, sc2, g2 = ada_terms(cond_lat, w_ada[l], b_ada[l])
        ctx_terms = ada_terms(cond_ctx, w_ada[l], b_ada[l])

        z_ctx = modulate(rms_norm(ctx, norm1_g[l]), ctx_terms[0], ctx_terms[1])
        ctx_proj = z_ctx @ (p["w_in"][:, :KV_COLS] if last else p["w_in"])
        ctx_keys = mixer_keys(ctx_proj[..., :KV_COLS], p, None)

        z_lat = modulate(rms_norm(x, norm1_g[l]), sh1, sc1)
        lat_proj = z_lat @ p["w_in"]
        lat_keys = mixer_keys(lat_proj[..., :KV_COLS], p, rope)
        keys = tuple(jnp.concatenate([ck, lk], axis=1) for ck, lk in zip(ctx_keys, lat_keys))
        x = x + g1 * attend_and_merge(lat_proj, keys, p, rope)
        x = x + g2 * conv_ffn(modulate(rms_norm(x, norm2_g[l]), sh2, sc2), p)

        if not last:
            ctx = ctx + ctx_terms[2] * attend_and_merge(ctx_proj, ctx_keys, p, None)
            z2 = modulate(rms_norm(ctx, norm2_g[l]), ctx_terms[3], ctx_terms[4])
            ctx = ctx + ctx_terms[5] * conv_ffn(z2, p)

    return rms_norm(x, final_norm_g)
```

```python
import numpy as np
from contextlib import ExitStack
import concourse.bass as bass
import concourse.mybir as mybir
from concourse.bass_utils import run_bass_kernel_spmd

F32 = mybir.dt.float32
BF16 = mybir.dt.bfloat16
AF = mybir.ActivationFunctionType
ALU = mybir.AluOpType

D = 2048
T = 2048
CTX = 256
TK = T + CTX
NKT = TK // 128
GRID_W = 64
EPS = 1e-6
KV_COLS = 1088
Q_COLS = 1792
IN_COLS = 6976
DFF = 5632
NFC = DFF // 128
TB = 512
NB = T // TB
DEBUG = {}


class Buf:
    __slots__ = ("name", "last_write", "readers", "dsem", "dcount", "excl")

    def __init__(self, name):
        self.name = name
        self.excl = False
        self.last_write = None
        self.readers = {}
        self.dsem = None
        self.dcount = 0


class Rec:
    __slots__ = ("eng", "fn", "deps", "signal", "sigval", "kind", "dsem", "dval")

    def __init__(self, eng, fn, deps, kind):
        self.eng = eng
        self.fn = fn
        self.deps = deps
        self.signal = False
        self.sigval = None
        self.kind = kind
        self.dsem = None
        self.dval = None


class Prog:
    ENGS = ("tensor", "vector", "scalar", "gpsimd", "sync")

    def __init__(self, nc, stack):
        self.nc = nc
        self.stack = stack
        self.recs = {e: [] for e in self.ENGS}
        self.esem = {e: stack.enter_context(nc.semaphore("es_" + e)) for e in self.ENGS}
        self.nsem = 0
        self.pending_barrier = {e: [] for e in self.ENGS}
        self.dmas_since_barrier = []

    def buf(self, name):
        return Buf(name)

    def bufs(self, name, n):
        return [Buf(f"{name}{i}") for i in range(n)]

    def _collect(self, reads, writes, sem_skip=None, eng=None):
        deps = []
        for b in reads:
            if b.last_write is not None:
                deps.append(b.last_write)
            if b.excl:
                for e2, t in b.readers.items():
                    if e2 != eng:
                        deps.append(t)
        for b in writes:
            lw = b.last_write
            if lw is not None and not (sem_skip is not None and lw.kind == "d" and lw.dsem is sem_skip):
                deps.append(lw)
            for t in b.readers.values():
                deps.append(t)
        return deps

    def op(self, eng, fn, reads=(), writes=()):
        deps = self._collect(reads, writes, eng=eng)
        if self.pending_barrier[eng]:
            deps.extend(self.pending_barrier[eng])
            self.pending_barrier[eng] = []
        r = Rec(eng, fn, deps, "c")
        self.recs[eng].append(r)
        for b in reads:
            b.readers[eng] = r
        for b in writes:
            b.last_write = r
            b.readers = {}
        return r

    def dma(self, eng, fn, reads=(), writes=(), sembuf=None):
        sb = sembuf or (writes[0] if writes else reads[0])
        if sb.dsem is None:
            sb.dsem = self.stack.enter_context(self.nc.semaphore(f"ds{self.nsem}"))
            self.nsem += 1
        deps = self._collect(reads, writes, sem_skip=sb.dsem)
        if self.pending_barrier[eng]:
            deps.extend(self.pending_barrier[eng])
            self.pending_barrier[eng] = []
        r = Rec(eng, fn, deps, "d")
        sb.dcount += 1
        r.dsem = sb.dsem
        r.dval = 16 * sb.dcount
        self.recs[eng].append(r)
        self.dmas_since_barrier.append(r)
        for b in reads:
            b.readers["dma:" + str(id(sb))] = r
        for b in writes:
            b.last_write = r
            b.readers = {}
        return r

    def barrier(self):
        toks = []
        for e in self.ENGS:
            for r in reversed(self.recs[e]):
                if r.kind == "c":
                    toks.append(r)
                    break
        latest = {}
        for r in self.dmas_since_barrier:
            latest[id(r.dsem)] = r
        toks.extend(latest.values())
        self.dmas_since_barrier = []
        for e in self.ENGS:
            self.pending_barrier[e] = list(toks)

    def emit(self, final_bufs=()):
        nc = self.nc
        same_ok = ("tensor",)
        for e in self.ENGS:
            for r in self.recs[e]:
                for d in r.deps:
                    if d.kind == "c":
                        if d.eng == r.eng and r.kind == "c" and d.eng in same_ok:
                            continue
                        d.signal = True
        for e in self.ENGS:
            c = 0
            for r in self.recs[e]:
                if r.kind == "c" and r.signal:
                    c += 1
                    r.sigval = c
        self.nwaits = 0
        all_sems = [self.esem[e] for e in self.ENGS]
        seen = set()
        for e in self.ENGS:
            for r in self.recs[e]:
                if r.kind == "d" and id(r.dsem) not in seen:
                    seen.add(id(r.dsem))
                    all_sems.append(r.dsem)
        for bf in final_bufs:
            if bf.dsem is not None and id(bf.dsem) not in seen:
                seen.add(id(bf.dsem))
                all_sems.append(bf.dsem)
        with nc.Block() as block0:
            def clr(h):
                for sm in all_sems:
                    h.sem_clear(sm)
            block0.gpsimd(clr)
        with nc.Block() as block:
            for e in self.ENGS:
                def body(h, e=e):
                    waited = {}
                    for r in self.recs[e]:
                        need = {}
                        for d in r.deps:
                            if d.kind == "c":
                                if d.eng == r.eng and r.kind == "c" and d.eng in same_ok:
                                    continue
                                key = ("c", d.eng)
                                sem, val = self.esem[d.eng], d.sigval
                            else:
                                key = ("d", id(d.dsem))
                                sem, val = d.dsem, d.dval
                            if waited.get(key, 0) >= val:
                                continue
                            if key not in need or need[key][1] < val:
                                need[key] = (sem, val)
                        for key, (sem, val) in need.items():
                            h.wait_ge(sem, val)
                            waited[key] = val
                            self.nwaits += 1
                        inst = r.fn(h)
                        if r.kind == "d":
                            inst.then_inc(r.dsem, 16)
                        elif r.signal:
                            inst.then_inc(self.esem[e], 1)
                    if e == "sync":
                        for b in final_bufs:
                            h.wait_ge(b.dsem, 16 * b.dcount)
                getattr(block, e)(body)


class _Stop(Exception):
    pass


def build_program(dbg=False, stop=99):
    nc = bass.Bass("TRN2", target_bir_lowering=False)
    try:
        return _build(nc, dbg, stop)
    except _Stop as e:
        return nc, e.args[0]


def _build(nc, dbg, stop):

    def din(name, shape, dt=F32):
        return nc.dram_tensor(name, list(shape), dt, kind="ExternalInput").ap()

    x_d = din("x", [T, D])
    ctx_d = din("ctx", [CTX, D])
    cvec_d = din("cvec", [128, 32])
    wada_d = din("w_ada", [D, 6 * D])
    bada_d = din("badaT", [128, 96])
    n1g_d = din("n1g", [128, 16])
    n2g_d = din("n2g", [128, 16])
    win_d = din("w_in", [D, IN_COLS])
    gq_d = din("gq", [768])
    wqup_d = din("w_q_up", [768, 1536])
    gkv_d = din("gkv", [512])
    wkvup_d = din("w_kv_up", [512, 2048])
    gqq_d = din("gqq", [128])
    gqk_d = din("gqk", [128])
    wbra_d = din("w_br_a", [1024, D])
    wbrb_d = din("w_br_b", [1024, D])
    wout_d = din("w_out", [D, D])
    wup_d = din("w_up", [D, 2 * DFF])
    convp_d = din("convp", [128, 88 * 4])
    wdown_d = din("w_down", [DFF, D])
    gfin_d = din("gfin", [D])
    identf_d = din("identf", [128, 128])
    ropeAC_d = din("ropeAC", [T, 512])
    ropeAS_d = din("ropeAS", [T, 512])
    ropeBC_d = din("ropeBC", [T, 128])
    ropeBS_d = din("ropeBS", [T, 128])
    out_d = nc.dram_tensor("out", [T, D], F32, kind="ExternalOutput").ap()
    x1s_d = nc.dram_tensor("x1s", [T, D], F32, kind="Internal").ap()
    kas_d = nc.dram_tensor("kas", [10, 128, TK], BF16, kind="Internal").ap()
    vas_d = nc.dram_tensor("vas", [10, 128, NKT * 128], BF16, kind="Internal").ap()
    dbg_outs = {}

    with ExitStack() as st:
        P = Prog(nc, st)

        def sb(name, shape, dt):
            return st.enter_context(nc.sbuf_tensor("s_" + name, list(shape), dt))

        def ps(name, shape, dt):
            return st.enter_context(nc.psum_tensor("p_" + name, list(shape), dt))

        identf = sb("identf", [128, 128], F32); Bidf = P.buf("identf")
        identb = sb("identb", [128, 128], BF16); Bidb = P.buf("identb")
        onesb = sb("onesb", [128, 128], BF16); Bones = P.buf("onesb")
        epsc = sb("epsc", [128, 1], F32); Beps = P.buf("eps")
        cvec = sb("cvec", [128, 32], F32); Bcvec = P.buf("cvec")
        csb = sb("csb", [128, 32], BF16); Bcsb = P.buf("csb")
        adaT = sb("adaT", [128, 96, 2], F32); Bada = P.buf("adaT")
        badaT = sb("badaT", [128, 96], F32); Bbada = P.buf("badaT")
        n1g = sb("n1g", [128, 16], F32); Bn1g = P.buf("n1g")
        n2g = sb("n2g", [128, 16], F32); Bn2g = P.buf("n2g")
        a1 = sb("a1", [128, 16], F32)
        a1c = sb("a1c", [128, 16], F32)
        a2 = sb("a2", [128, 16], F32)
        Bmods = P.buf("mods")
        gq_b = sb("gq_b", [128, 768], F32); Bgq = P.buf("gq")
        gkv_b = sb("gkv_b", [128, 512], F32); Bgkv = P.buf("gkv")
        gqq_b = sb("gqq_b", [128, 128], F32); Bgqq = P.buf("gqq")
        gqk_b = sb("gqk_b", [128, 128], F32); Bgqk = P.buf("gqk")
        convp = sb("convp", [128, 88, 4], F32); Bconv = P.buf("convp")
        kpeT = sb("kpeT", [128, TK], BF16); Bkpe = P.buf("kpeT")
        junk = sb("junk", [128, 2048], BF16); Bjunk = P.buf("junk")
        ybf = [sb(f"ybf{i}", [128, 2048], BF16) for i in range(2)]; Bybf = P.bufs("ybf", 2)
        Bxst = P.bufs("xst", 2)
        xst = [None, None]
        sst_t = [sb(f"sst{i}", [128, 16], F32) for i in range(4)]; Bsst = P.bufs("sst", 4)
        NSTAT = 8
        stat_t = [sb(f"stat{i}", [128, 16], F32) for i in range(NSTAT)]; Bstat = P.bufs("stat", NSTAT)
        rtabB = [sb(f"rtabB{i}", [128, 384], F32) for i in range(4)]; BrtabB = P.bufs("rtabB", 4)
        rtabA = [sb(f"rtabA{i}", [128, 1024], F32) for i in range(1)]; BrtabA = P.bufs("rtabA", 1)
        WSLOT = 2048
        NSTG, NWB = 4, 4
        wstg = [sb(f"wstg{i}", [128, WSLOT], F32) for i in range(NSTG)]; Bwstg = P.bufs("wstg", NSTG)
        wbf = [sb(f"wbf{i}", [128, WSLOT], BF16) for i in range(NWB)]; Bwbf = P.bufs("wbf", NWB)
        ARENA_W = 30720
        arena = sb("arena", [128, ARENA_W], F32)

        pf = [ps(f"pf{i}", [128, 512], F32) for i in range(6)]; Bpf = P.bufs("pf", 6)
        pb = [ps(f"pb{i}", [128, 1024], BF16) for i in range(2)]; Bpb = P.bufs("pb", 2)
        for b_ in Bpf + Bpb:
            b_.excl = True

        cnt = {"unit": 0, "stat": 0, "w": 0, "wb": 0, "cast": 0, "evac": 0, "pf": 0, "rt": 0, "rtA": 0, "ybf": 0, "xst": 0}

        def stat():
            i = cnt["stat"] % NSTAT
            cnt["stat"] += 1
            return stat_t[i], Bstat[i]

        class Arena:
            def __init__(self):
                self.off = 0

            def reset(self):
                self.off = 0

            def f32(self, n):
                a = arena[:, self.off:self.off + n]
                self.off += n
                assert self.off <= ARENA_W, self.off
                return a

            def bf(self, n):
                w = (n + 1) // 2
                a = arena[:, self.off:self.off + w].bitcast(BF16)
                self.off += w
                assert self.off <= ARENA_W, self.off
                return a

        AR = Arena()

        def dbg_out(name, ap, shape, Bsrc, dt=F32):
            if not dbg:
                return
            d = nc.dram_tensor("dbg_" + name, list(shape), dt, kind="ExternalOutput").ap()
            dbg_outs[name] = (list(shape), dt)
            b = P.buf("dbg_" + name)
            P.dma("sync", lambda h: h.dma_start(out=d, in_=ap), reads=[Bsrc], sembuf=b)
            dbg_bufs.append(b)

        dbg_bufs = []

        def stop_at(level):
            if stop < level:
                P.emit(final_bufs=dbg_bufs)
                raise _Stop(dbg_outs)

        cast_engs = ["vector", "scalar"]

        def load_w(parts, dst=None, Bdst=None):
            i = cnt["w"]
            cnt["w"] += 1
            si = i % NSTG
            off = 0
            for ap in parts:
                shp = list(ap.shape)
                n = 1
                for s in shp[1:]:
                    n *= s
                if len(shp) == 3:
                    dstv = wstg[si][:, off:off + n].rearrange("p (a b) -> p a b", a=shp[1])
                else:
                    dstv = wstg[si][:, off:off + n]
                P.dma("sync", lambda h, dstv=dstv, ap=ap: h.dma_start(out=dstv, in_=ap), writes=[Bwstg[si]])
                off += n
            assert off <= WSLOT
            if dst is None:
                bi = cnt["wb"] % NWB
                cnt["wb"] += 1
                dst, Bdst = wbf[bi][:, 0:off], Bwbf[bi]
            eng = cast_engs[cnt["cast"] % len(cast_engs)]
            cnt["cast"] += 1
            if eng == "scalar":
                P.op(eng, lambda h, dst=dst, si=si, off=off: h.copy(out=dst, in_=wstg[si][:, 0:off]),
                     reads=[Bwstg[si]], writes=[Bdst])
            else:
                P.op(eng, lambda h, dst=dst, si=si, off=off: h.tensor_copy(out=dst, in_=wstg[si][:, 0:off]),
                     reads=[Bwstg[si]], writes=[Bdst])
            return dst, Bdst

        def wrows(w_d, k0, nk, c0, ncol):
            return w_d[k0 * 128:(k0 + nk) * 128, c0:c0 + ncol].rearrange("(k p) n -> p k n", p=128)

        class WStream:
            def __init__(self, specs):
                self.specs = specs
                self.tiles = []
                self.consumed = 0
                self.released = 0
                self.prime()

            def prime(self):
                while len(self.tiles) < len(self.specs) and len(self.tiles) - self.released < NWB:
                    self.tiles.append(load_w(self.specs[len(self.tiles)]))

            def next(self):
                assert self.consumed < len(self.tiles), "weight stream underrun (call done())"
                t = self.tiles[self.consumed]
                self.consumed += 1
                return t

            def done(self):
                self.released = self.consumed
                self.prime()

        def run_pipeline(units, look=2):
            res = {}
            n = len(units)
            for i in range(min(look, n)):
                res[i] = units[i][0]()
            for i in range(n):
                if i + look < n:
                    res[i + look] = units[i + look][0]()
                units[i][1](*res.pop(i))

        def next_pf(pool=(0, 1, 2, 3, 4, 5)):
            i = pool[cnt["pf"] % len(pool)]
            cnt["pf"] += 1
            return pf[i], Bpf[i]

        def evac_eng():
            e = ("scalar", "vector")[cnt["evac"] % 2]
            cnt["evac"] += 1
            return e

        def rstd_from_ss(stt, Bs, np_, n, ncol=1, c_ss=0, c_tmp=4, c_out=8):
            P.op("scalar", lambda h: h.activation(out=stt[0:np_, c_tmp:c_tmp + ncol], in_=stt[0:np_, c_ss:c_ss + ncol],
                                                  func=AF.Sqrt, scale=1.0 / n, bias=epsc[0:np_, :]),
                 reads=[Bs, Beps], writes=[Bs])
            P.op("vector", lambda h: h.reciprocal(out=stt[0:np_, c_out:c_out + ncol], in_=stt[0:np_, c_tmp:c_tmp + ncol]),
                 reads=[Bs], writes=[Bs])

        def norm_tile_T(xt, Bx, np_, mulfn, addfn, zT, col0, Bz):
            stt, Bs = stat()
            yi = cnt["ybf"] % 2
            cnt["ybf"] += 1
            y, By = ybf[yi], Bybf[yi]
            P.op("scalar", lambda h: h.activation(out=junk[0:np_, :], in_=xt, func=AF.Square, accum_out=stt[0:np_, 0:1]),
                 reads=[Bx], writes=[Bjunk, Bs])
            rstd_from_ss(stt, Bs, np_, D)
            P.op("vector", lambda h: h.tensor_scalar(out=y[0:np_, :], in0=xt, scalar1=stt[0:np_, 8:9], scalar2=None, op0=ALU.mult),
                 reads=[Bx, Bs], writes=[By])
            for half in range(2):
                for k8 in range(8):
                    kc = half * 8 + k8
                    P.op("tensor", lambda h, half=half, k8=k8, kc=kc: h.transpose(
                        out=pb[half][:, k8 * 128:k8 * 128 + np_], in_=y[0:np_, kc * 128:(kc + 1) * 128],
                        identity=identb[0:np_, 0:np_]), reads=[By, Bidb], writes=[Bpb[half]])
                for k8 in range(8):
                    kc = half * 8 + k8
                    e = ("scalar", "vector")[half]
                    src = pb[half][:, k8 * 128:k8 * 128 + np_]
                    dst = zT[:, kc, col0:col0 + np_]
                    if e == "scalar":
                        P.op("scalar", lambda h, src=src, dst=dst, kc=kc: h.activation(
                            out=dst, in_=src, func=AF.Identity, scale=mulfn(kc), bias=addfn(kc)),
                            reads=[Bpb[half], Bmods, Bada], writes=[Bz])
                    else:
                        P.op("vector", lambda h, src=src, dst=dst, kc=kc: h.tensor_scalar(
                            out=dst, in0=src, scalar1=mulfn(kc), scalar2=addfn(kc), op0=ALU.mult, op1=ALU.add),
                            reads=[Bpb[half], Bmods, Bada], writes=[Bz])

        def load_x_tile(src_ap, np_=128):
            i = cnt["xst"] % 2
            cnt["xst"] += 1
            dst = xst[i][0:np_, :]
            P.dma("sync", lambda h: h.dma_start(out=dst, in_=src_ap), writes=[Bxst[i]])
            return dst, Bxst[i]

        def load_rope(t0):
            i = cnt["rt"] % 4
            cnt["rt"] += 1
            r, Br = rtabB[i], BrtabB[i]
            P.dma("sync", lambda h: h.dma_start(out=r[:, 0:128], in_=ropeBC_d[t0:t0 + 128, :]), writes=[Br])
            P.dma("sync", lambda h: h.dma_start(out=r[:, 128:256], in_=ropeBS_d[t0:t0 + 128, :]), writes=[Br])
            P.dma("sync", lambda h: h.dma_start(out=r[:, 256:320], in_=ropeAC_d[t0:t0 + 128, 0:64]), writes=[Br])
            P.dma("sync", lambda h: h.dma_start(out=r[:, 320:384], in_=ropeAS_d[t0:t0 + 128, 0:64]), writes=[Br])
            return {"AC": r[:, 256:320], "AS": r[:, 320:384], "BC": r[:, 0:128], "BS": r[:, 128:256], "B": Br}

        def load_ropeA(t0):
            i = cnt["rtA"] % 1
            cnt["rtA"] += 1
            r, Br = rtabA[i], BrtabA[i]
            P.dma("sync", lambda h: h.dma_start(out=r[:, 0:512], in_=ropeAC_d[t0:t0 + 128, :]), writes=[Br])
            P.dma("sync", lambda h: h.dma_start(out=r[:, 512:1024], in_=ropeAS_d[t0:t0 + 128, :]), writes=[Br])
            return {"AC": r[:, 0:512], "AS": r[:, 512:1024], "B": Br}

        def rope(src, Bsrc, W, q, Ct, St, Brt, tmp_t, tmp_u, Btmp, out_ap, Bout):
            g = W // (2 * q)

            def v(ap):
                return ap.rearrange("p (g h i) -> p g h i", g=g, h=2)
            P.op("gpsimd", lambda h: h.tensor_tensor(out=tmp_t, in0=src, in1=Ct, op=ALU.mult),
                 reads=[Bsrc, Brt], writes=[Btmp])
            P.op("vector", lambda h: h.tensor_tensor(out=v(tmp_u)[:, :, 0, :], in0=v(src)[:, :, 1, :], in1=v(St)[:, :, 0, :], op=ALU.mult),
                 reads=[Bsrc, Brt], writes=[Btmp])
            P.op("vector", lambda h: h.tensor_tensor(out=v(tmp_u)[:, :, 1, :], in0=v(src)[:, :, 0, :], in1=v(St)[:, :, 1, :], op=ALU.mult),
                 reads=[Bsrc, Brt], writes=[Btmp])
            P.op("vector", lambda h: h.tensor_tensor(out=out_ap, in0=tmp_t, in1=tmp_u, op=ALU.add),
                 reads=[Btmp], writes=[Bout])

        def transpose_to(src_bf, Bsrc, np_in, nf, dst, Bdst, bank_sel):
            half = bank_sel % 2
            P.op("tensor", lambda h: h.transpose(out=pb[half][0:nf, 0:np_in], in_=src_bf, identity=identb[0:np_in, 0:np_in]),
                 reads=[Bsrc, Bidb], writes=[Bpb[half]])
            e = evac_eng()
            if e == "scalar":
                P.op("scalar", lambda h: h.copy(out=dst, in_=pb[half][0:nf, 0:np_in]), reads=[Bpb[half]], writes=[Bdst])
            else:
                P.op("vector", lambda h: h.tensor_copy(out=dst, in_=pb[half][0:nf, 0:np_in]), reads=[Bpb[half]], writes=[Bdst])

        P.dma("sync", lambda h: h.dma_start(out=identf[:], in_=identf_d), writes=[Bidf])
        P.dma("sync", lambda h: h.dma_start(out=cvec[:], in_=cvec_d), writes=[Bcvec])
        P.dma("sync", lambda h: h.dma_start(out=badaT[:], in_=bada_d), writes=[Bbada])
        P.dma("sync", lambda h: h.dma_start(out=n1g[:], in_=n1g_d), writes=[Bn1g])
        P.dma("sync", lambda h: h.dma_start(out=n2g[:], in_=n2g_d), writes=[Bn2g])
        P.dma("sync", lambda h: h.dma_start(out=convp[:].rearrange("p a b -> p (a b)"), in_=convp_d), writes=[Bconv])
        P.dma("sync", lambda h: h.dma_start(out=gq_b[:], in_=gq_d.partition_broadcast(128)), writes=[Bgq])
        P.dma("sync", lambda h: h.dma_start(out=gkv_b[:], in_=gkv_d.partition_broadcast(128)), writes=[Bgkv])
        P.dma("sync", lambda h: h.dma_start(out=gqq_b[:], in_=gqq_d.partition_broadcast(128)), writes=[Bgqq])
        P.dma("sync", lambda h: h.dma_start(out=gqk_b[:], in_=gqk_d.partition_broadcast(128)), writes=[Bgqk])
        P.op("gpsimd", lambda h: h.tensor_copy(out=identb[:], in_=identf[:]), reads=[Bidf], writes=[Bidb])
        P.op("gpsimd", lambda h: h.memset(onesb[:], 1.0), writes=[Bones])
        P.op("gpsimd", lambda h: h.memset(epsc[:], EPS), writes=[Beps])

        if stop < 0:
            dbg_out("gq", gq_b[:], [128, 768], Bgq)
            dbg_out("conv", convp[:].rearrange("p a b -> p (a b)"), [128, 352], Bconv)
            dbg_out("idb", identb[:], [128, 128], Bidb, BF16)
            P.emit(final_bufs=dbg_bufs)
            return nc, dbg_outs
        P.op("scalar", lambda h: h.activation(out=csb[:], in_=cvec[:], func=AF.Silu), reads=[Bcvec], writes=[Bcsb])
        pada, Bpada = pf[5], Bpf[5]
        ws0 = WStream([[wrows(wada_d, 0, 16, t * 128, 128)] for t in range(96)])
        for t in range(96):
            wt, Bw = ws0.next()
            wv = wt.rearrange("p (k n) -> p k n", k=16)
            for kc in range(16):
                P.op("tensor", lambda h, wv=wv, kc=kc, t=t: h.matmul(
                    pada[:, t * 2:t * 2 + 2], lhsT=wv[:, kc, :], rhs=csb[:, kc * 2:kc * 2 + 2],
                    start=(kc == 0), stop=(kc == 15)), reads=[Bw, Bcsb], writes=[Bpada])
            ws0.done()
        padav = pada[:, 0:192].rearrange("p (j m) -> p j m", m=2)
        for m in range(2):
            P.op("vector", lambda h, m=m: h.tensor_tensor(out=adaT[:, :, m], in0=padav[:, :, m], in1=badaT[:], op=ALU.add),
                 reads=[Bpada, Bbada], writes=[Bada])
        P.op("vector", lambda h: h.scalar_tensor_tensor(out=a1[:], in0=adaT[:, 16:32, 0], scalar=1.0, in1=n1g[:], op0=ALU.add, op1=ALU.mult),
             reads=[Bada, Bn1g], writes=[Bmods])
        P.op("vector", lambda h: h.scalar_tensor_tensor(out=a1c[:], in0=adaT[:, 16:32, 1], scalar=1.0, in1=n1g[:], op0=ALU.add, op1=ALU.mult),
             reads=[Bada, Bn1g], writes=[Bmods])
        P.op("vector", lambda h: h.scalar_tensor_tensor(out=a2[:], in0=adaT[:, 64:80, 0], scalar=1.0, in1=n2g[:], op0=ALU.add, op1=ALU.mult),
             reads=[Bada, Bn2g], writes=[Bmods])
        dbg_out("adaT", adaT[:].rearrange("p a b -> p (a b)"), [128, 192], Bada)

        def ada(term, kc, m=0):
            return adaT[:, term * 16 + kc, m:m + 1]

        if stop < 1:
            P.emit(final_bufs=dbg_bufs)
            return nc, dbg_outs
        P.barrier()
        AR.reset()
        cast_engs[:] = ["vector", "scalar"]
        xst[0] = AR.f32(D); xst[1] = AR.f32(D)
        zT = AR.bf(16 * TB).rearrange("p (k t) -> p k t", k=16); BzT = P.buf("zT")
        ckvT = AR.bf(4 * TB).rearrange("p (k t) -> p k t", k=4); BckvT = P.buf("ckvT")
        wkup = AR.bf(4 * 1024).rearrange("p (k n) -> p k n", k=4); Bwkup = P.buf("wkup")
        wvup = AR.bf(4 * 1024).rearrange("p (k n) -> p k n", k=4); Bwvup = P.buf("wvup")
        kvn = [AR.bf(512) for _ in range(4)]; Bkvn = P.bufs("kvn", 4)
        ktok = [AR.bf(128) for _ in range(2)]; Bktok = P.bufs("ktok", 2)
        raw = [AR.f32(128) for _ in range(2)]; Braw = P.bufs("raw", 2)
        rt_t = [AR.f32(128) for _ in range(2)]
        rt_u = [AR.f32(128) for _ in range(2)]
        rt_o = [AR.f32(128) for _ in range(2)]
        Brt_tmp = P.bufs("rtmp", 2); Brt_o = P.bufs("rto", 2)
        vtok = [AR.bf(1024) for _ in range(2)]; Bvtok = P.bufs("vtok", 2)
        kblk = [AR.bf(TB) for _ in range(2)]; Bkblk = P.bufs("kblk", 2)
        print("phase1 arena words", AR.off)
        wkvv = wkvup_d.rearrange("(k p) (h t d) -> p k h t d", p=128, t=2, d=128)
        for kc in range(4):
            for hh in range(2):
                load_w([wkvv[:, kc, hh * 4:(hh + 1) * 4, 0, :]], dst=wkup[:, kc, hh * 512:(hh + 1) * 512], Bdst=Bwkup)
                load_w([wkvv[:, kc, hh * 4:(hh + 1) * 4, 1, :]], dst=wvup[:, kc, hh * 512:(hh + 1) * 512], Bdst=Bwvup)

        stop_at(0.1)
        kv_tiles = [(0, 128), (128, 128), (256, 128), (384, 128), (512, 64), (576, 128), (704, 128), (832, 128), (960, 128)]
        groups = [(0, 2, True)] + [(2 + 4 * g, 4, False) for g in range(4)]
        unit = 0
        ws1 = WStream([[wrows(win_d, 0, 16, c0, ncol)] for _g in groups for (c0, ncol) in kv_tiles])
        for (kt0, ntl, is_ctx) in groups:
            ntok = ntl * 128
            m = 1 if is_ctx else 0
            ropes = [] if is_ctx else [load_rope((kt0 - 2 + ti_) * 128) for ti_ in range(ntl)]
            for ti in range(ntl):
                if is_ctx:
                    src = ctx_d[ti * 128:(ti + 1) * 128, :]
                else:
                    src = x_d[(kt0 - 2 + ti) * 128:(kt0 - 2 + ti + 1) * 128, :]
                xt, Bx = load_x_tile(src)
                mulfn = (lambda kc: a1c[:, kc:kc + 1]) if is_ctx else (lambda kc: a1[:, kc:kc + 1])
                addfn = (lambda kc, m=m: ada(0, kc, m))
                norm_tile_T(xt, Bx, 128, mulfn, addfn, zT, ti * 128, BzT)
            if dbg and kt0 == 2:
                dbg_out("zT", zT[:, 0, :], [128, TB], BzT, BF16)
            stop_at(0.2)
            sstats = [(sst_t[i_], Bsst[i_]) for i_ in range(ntl)]
            units = []
            cur = {}
            for ci, (c0, ncol) in enumerate(kv_tiles):
                for ti in range(ntl):
                    def mm(ci=ci, ncol=ncol, ti=ti, ntl=ntl):
                        if ti == 0:
                            wt, Bw = ws1.next()
                            cur["w"] = (wt.rearrange("p (k n) -> p k n", k=16), Bw)
                        wv, Bw = cur["w"]
                        pt, Bp = next_pf((0, 1, 2, 3))
                        for kc in range(16):
                            P.op("tensor", lambda h, pt=pt, wv=wv, kc=kc, ti=ti, ncol=ncol: h.matmul(
                                pt[:, 0:ncol], lhsT=zT[:, kc, ti * 128:(ti + 1) * 128], rhs=wv[:, kc, :],
                                start=(kc == 0), stop=(kc == 15)), reads=[BzT, Bw], writes=[Bp])
                        if ti == ntl - 1:
                            ws1.done()
                        return pt, Bp

                    def post(pt, Bp, ci=ci, ti=ti, kt0=kt0, ntl=ntl, is_ctx=is_ctx, ropes=ropes, sstats=sstats, ntok=ntok):
                        kt = kt0 + ti
                        stt, Bs = sstats[ti]
                        u2 = cnt["unit"] % 2
                        cnt["unit"] += 1
                        unit = cnt["unit"]
                        if ci < 4:
                            P.op("scalar", lambda h, pt=pt, stt=stt, ci=ci: h.activation(
                                out=junk[:, 0:128], in_=pt[:, 0:128], func=AF.Square, accum_out=stt[:, ci:ci + 1]),
                                reads=[Bp], writes=[Bjunk, Bs])
                            P.op("vector", lambda h, pt=pt, ti=ti, ci=ci: h.tensor_tensor(
                                out=kvn[ti][:, ci * 128:(ci + 1) * 128], in0=pt[:, 0:128], in1=gkv_b[:, ci * 128:(ci + 1) * 128], op=ALU.mult),
                                reads=[Bp, Bgkv], writes=[Bkvn[ti]])
                            if ci == 3:
                                P.op("scalar", lambda h, stt=stt: h.activation(out=stt[:, 12:16], in_=stt[:, 0:4], func=AF.Identity, accum_out=stt[:, 5:6]),
                                     reads=[Bs], writes=[Bs])
                                rstd_from_ss(stt, Bs, 128, 512, c_ss=5, c_tmp=6, c_out=8)
                                P.op("vector", lambda h, ti=ti, stt=stt: h.tensor_scalar(
                                    out=kvn[ti][:], in0=kvn[ti][:], scalar1=stt[:, 8:9], scalar2=None, op0=ALU.mult),
                                    reads=[Bs, Bkvn[ti]], writes=[Bkvn[ti]])
                                for c4 in range(4):
                                    transpose_to(kvn[ti][:, c4 * 128:(c4 + 1) * 128], Bkvn[ti], 128, 128,
                                                 ckvT[:, c4, ti * 128:(ti + 1) * 128], BckvT, c4)
                        elif ci == 4:
                            if is_ctx:
                                P.op("scalar", lambda h, pt=pt, u2=u2: h.copy(out=ktok[u2][:, 0:64], in_=pt[:, 0:64]), reads=[Bp], writes=[Bktok[u2]])
                            else:
                                if ti >= len(ropes):
                                    ropes.append(load_rope((kt - 2) * 128))
                                rp = ropes[ti]
                                P.op("scalar", lambda h, pt=pt, u2=u2: h.activation(out=raw[u2][:, 0:64], in_=pt[:, 0:64], func=AF.Identity), reads=[Bp], writes=[Braw[u2]])
                                rope(raw[u2][:, 0:64], Braw[u2], 64, 16, rp["AC"], rp["AS"], rp["B"],
                                     rt_t[u2][:, 0:64], rt_u[u2][:, 0:64], Brt_tmp[u2], rt_o[u2][:, 0:64], Brt_o[u2])
                                P.op("scalar", lambda h, u2=u2: h.copy(out=ktok[u2][:, 0:64], in_=rt_o[u2][:, 0:64]), reads=[Brt_o[u2]], writes=[Bktok[u2]])
                                if dbg and kt == 2:
                                    dbg_out("kp_raw", raw[u2][:, 0:64], [128, 64], Braw[u2])
                                    dbg_out("kp_t", rt_t[u2][:, 0:64], [128, 64], Brt_tmp[u2])
                                    dbg_out("kp_u", rt_u[u2][:, 0:64], [128, 64], Brt_tmp[u2])
                                    dbg_out("kp_o", rt_o[u2][:, 0:64], [128, 64], Brt_o[u2])
                                    dbg_out("kp_tab", rp["AC"], [128, 64], rp["B"])
                            P.op("gpsimd", lambda h, u2=u2: h.tensor_copy(out=ktok[u2][:, 64:128], in_=ktok[u2][:, 0:64]), reads=[Bktok[u2]], writes=[Bktok[u2]])
                            transpose_to(ktok[u2][:, 0:128], Bktok[u2], 128, 128, kpeT[:, kt * 128:(kt + 1) * 128], Bkpe, unit)
                        elif ci in (5, 6):
                            hb = ci - 5
                            st2, Bs2 = stat()
                            P.op("scalar", lambda h, pt=pt, st2=st2: h.activation(
                                out=junk[:, 0:128], in_=pt[:, 0:128], func=AF.Square, accum_out=st2[:, 0:1]),
                                reads=[Bp], writes=[Bjunk, Bs2])
                            rstd_from_ss(st2, Bs2, 128, 128)
                            P.op("vector", lambda h, pt=pt, u2=u2: h.tensor_tensor(out=raw[u2][:], in0=pt[:, 0:128], in1=gqk_b[:], op=ALU.mult),
                                 reads=[Bp, Bgqk], writes=[Braw[u2]])
                            if is_ctx:
                                P.op("scalar", lambda h, u2=u2, st2=st2: h.activation(out=ktok[u2][:], in_=raw[u2][:], func=AF.Identity, scale=st2[:, 8:9]),
                                     reads=[Braw[u2], Bs2], writes=[Bktok[u2]])
                            else:
                                rp = ropes[ti]
                                rope(raw[u2][:], Braw[u2], 128, 32, rp["BC"], rp["BS"], rp["B"],
                                     rt_t[u2][:], rt_u[u2][:], Brt_tmp[u2], rt_o[u2][:], Brt_o[u2])
                                P.op("scalar", lambda h, u2=u2, st2=st2: h.activation(out=ktok[u2][:], in_=rt_o[u2][:], func=AF.Identity, scale=st2[:, 8:9]),
                                     reads=[Brt_o[u2], Bs2], writes=[Bktok[u2]])
                            kb = kblk[hb]
                            transpose_to(ktok[u2][:], Bktok[u2], 128, 128, kb[:, ti * 128:(ti + 1) * 128], Bkblk[hb], unit)
                            if ti == ntl - 1:
                                P.dma("scalar", lambda h, hb=hb, kb=kb, kt0=kt0, ntok=ntok: h.dma_start(
                                    out=kas_d[8 + hb, :, kt0 * 128:kt0 * 128 + ntok], in_=kb[:, 0:ntok]), reads=[Bkblk[hb]], sembuf=Bkblk[hb])
                        else:
                            hb = ci - 7
                            vt = vtok[u2]
                            P.op("scalar", lambda h, pt=pt, vt=vt: h.copy(out=vt[:, 0:128], in_=pt[:, 0:128]), reads=[Bp], writes=[Bvtok[u2]])
                            P.dma("scalar", lambda h, hb=hb, vt=vt, kt=kt: h.dma_start(
                                out=vas_d[8 + hb, :, kt * 128:(kt + 1) * 128], in_=vt[:, 0:128]), reads=[Bvtok[u2]], sembuf=Bvtok[u2])
                    units.append((mm, post))
            run_pipeline(units)
            stop_at(0.4)
            for hh in range(8):
                pt, Bp = next_pf((0, 1, 2, 3))
                for kc in range(4):
                    P.op("tensor", lambda h, pt=pt, kc=kc, hh=hh, ntok=ntok: h.matmul(
                        pt[:, 0:ntok], lhsT=wkup[:, kc, hh * 128:(hh + 1) * 128], rhs=ckvT[:, kc, 0:ntok],
                        start=(kc == 0), stop=(kc == 3)), reads=[Bwkup, BckvT], writes=[Bp])
                kb = kblk[hh % 2]
                Bkb = Bkblk[hh % 2]
                e = evac_eng()
                if e == "scalar":
                    P.op("scalar", lambda h, pt=pt, kb=kb, ntok=ntok: h.copy(out=kb[:, 0:ntok], in_=pt[:, 0:ntok]), reads=[Bp], writes=[Bkb])
                else:
                    P.op("vector", lambda h, pt=pt, kb=kb, ntok=ntok: h.tensor_copy(out=kb[:, 0:ntok], in_=pt[:, 0:ntok]), reads=[Bp], writes=[Bkb])
                P.dma("scalar", lambda h, hh=hh, kb=kb, kt0=kt0, ntok=ntok: h.dma_start(
                    out=kas_d[hh, :, kt0 * 128:kt0 * 128 + ntok], in_=kb[:, 0:ntok]), reads=[Bkb], sembuf=Bkb)
            for ti in range(ntl):
                kt = kt0 + ti
                vt = vtok[ti % 2]
                Bvt = Bvtok[ti % 2]
                for half in range(2):
                    pt, Bp = next_pf((0, 1, 2, 3))
                    for kc in range(4):
                        P.op("tensor", lambda h, pt=pt, kc=kc, ti=ti, half=half: h.matmul(
                            pt[:, 0:512], lhsT=ckvT[:, kc, ti * 128:(ti + 1) * 128], rhs=wvup[:, kc, half * 512:(half + 1) * 512],
                            start=(kc == 0), stop=(kc == 3)), reads=[Bwvup, BckvT], writes=[Bp])
                    e = evac_eng()
                    if e == "scalar":
                        P.op("scalar", lambda h, pt=pt, vt=vt, half=half: h.copy(out=vt[:, half * 512:(half + 1) * 512], in_=pt[:, 0:512]), reads=[Bp], writes=[Bvt])
                    else:
                        P.op("vector", lambda h, pt=pt, vt=vt, half=half: h.tensor_copy(out=vt[:, half * 512:(half + 1) * 512], in_=pt[:, 0:512]), reads=[Bp], writes=[Bvt])
                P.dma("scalar", lambda h, vt=vt, kt=kt: h.dma_start(
                    out=vas_d[0:8, :, kt * 128:(kt + 1) * 128].rearrange("h p d -> p h d"),
                    in_=vt[:, 0:1024].rearrange("p (h d) -> p h d", h=8)), reads=[Bvt], sembuf=Bvt)
        if dbg:
            dbg_out("kpeT", kpeT[0:64, :], [64, TK], Bkpe, BF16)

        if stop < 2:
            P.emit(final_bufs=dbg_bufs)
            return nc, dbg_outs
        P.barrier()
        AR.reset()
        if dbg:
            for nm, ap_ in (("kas0", kas_d[0]), ("kas8", kas_d[8]), ("vas0", vas_d[0]), ("vas8", vas_d[8])):
                dbg_out(nm, ap_, [128, TK], P.buf("dummy_" + nm), BF16)
        cast_engs[:] = ["vector", "scalar"]
        xst[0] = AR.f32(D); xst[1] = xst[0]
        Bxst[1] = Bxst[0]
        zT2 = AR.bf(16 * TB).rearrange("p (k t) -> p k t", k=16); BzT2 = P.buf("zT2")
        qoff = AR.off
        qaT = AR.bf(8 * TB).rearrange("p (h t) -> p h t", h=8); BqaT = P.buf("qaT")
        qpT = AR.bf(4 * TB).rearrange("p (h t) -> p h t", h=4); BqpT = P.buf("qpT")
        qbT = AR.bf(8 * TB).rearrange("p (h t) -> p h t", h=8); BqbT = P.buf("qbT")
        mT = arena[:, qoff:qoff + 8 * TB].bitcast(BF16).rearrange("p (k t) -> p k t", k=16); BmT = P.buf("mT")
        oT = AR.bf(16 * TB).rearrange("p (h t) -> p h t", h=16); BoT = P.buf("oT")
        kst = [AR.bf(TK) for _ in range(2)]; Bkst = P.bufs("kst", 2)
        vst = [AR.bf(NKT * 128).rearrange("p (k d) -> p k d", k=NKT) for _ in range(2)]; Bvst = P.bufs("vst", 2)
        pT = [AR.bf(TB) for _ in range(4)]; BpT = P.bufs("pT", 4)
        cqg = AR.bf(4 * 768).rearrange("p (t n) -> p t n", t=4); Bcqg = P.bufs("cqg", 4)
        cqT = AR.bf(6 * TB).rearrange("p (k t) -> p k t", k=6); BcqT = P.buf("cqT")
        qatok = [AR.bf(1024) for _ in range(2)]; Bqatok = P.bufs("qatok", 2)
        qpe = [AR.f32(512) for _ in range(1)] * 2; Bqpe = P.bufs("qpe", 1) * 2
        qpe_t = [AR.f32(512) for _ in range(1)] * 2
        qpe_u = [AR.f32(512) for _ in range(1)] * 2
        qpe_o = [AR.f32(512) for _ in range(1)] * 2
        Bqpe_tmp = P.bufs("qpetmp", 1) * 2; Bqpe_o = P.bufs("qpeo", 1) * 2
        qpb = [AR.bf(512) for _ in range(1)] * 2; Bqpb = P.bufs("qpb", 1) * 2
        rawq = [AR.f32(128) for _ in range(2)]; Brawq = P.bufs("raw2", 2)
        rq_t = [AR.f32(128) for _ in range(2)]
        rq_u = [AR.f32(128) for _ in range(2)]
        rq_o = [AR.f32(128) for _ in range(2)]
        Brq_tmp = P.bufs("rtmp2", 2); Brq_o = P.bufs("rto2", 2)
        qtok = [AR.bf(128) for _ in range(2)]; Bqtok = P.bufs("qtok", 2)
        toff = AR.off
        tail = [AR.f32(TB) for _ in range(4)]
        rden = tail[0:2]; Brden = P.bufs("rden", 2)
        gsig = tail[0:2]; Bgsig = P.bufs("gsig", 2)
        mtmp = tail[2:4]; Bmtmp = P.bufs("mtmp", 2)
        aoT2 = tail[0:2]; BaoT2 = P.bufs("aoT2", 2)
        xres = [tail[2].rearrange("p (t d) -> p t d", t=4), tail[3].rearrange("p (t d) -> p t d", t=4)]; Bxres = P.bufs("xres", 2)
        print("phase2 arena words", AR.off)
        wqup_v = wqup_d.rearrange("(k p) n -> p k n", p=128)

        q_tiles = [(KV_COLS + i * 128, 128) for i in range(14)]
        GA0 = KV_COLS + Q_COLS
        specs2 = []
        for _qb in range(NB):
            for (c0, ncol) in q_tiles:
                specs2.append([wrows(win_d, 0, 16, c0, ncol)])
            for _ti in range(4):
                for c3 in range(4):
                    for kh in range(2):
                        specs2.append([wqup_v[:, kh * 3:(kh + 1) * 3, c3 * 384:(c3 + 1) * 384]])
            for j in range(16):
                specs2.append([wrows(wbra_d, 0, 8, j * 128, 128), wrows(wbrb_d, 0, 8, j * 128, 128)])
                specs2.append([wrows(win_d, 0, 16, GA0 + j * 128, 128)])
                specs2.append([wrows(win_d, 0, 16, GA0 + D + j * 128, 128)])
            for jo in range(16):
                specs2.append([wrows(wout_d, 0, 16, jo * 128, 128)])
        ws2 = WStream(specs2)
        SCA = 192.0 ** -0.5
        SCB = 128.0 ** -0.5
        for qb in range(NB):
            t0 = qb * TB
            ropes = []
            if qb > 0:
                P.barrier()
            for ti in range(4):
                xt, Bx = load_x_tile(x_d[t0 + ti * 128:t0 + (ti + 1) * 128, :])
                norm_tile_T(xt, Bx, 128, lambda kc: a1[:, kc:kc + 1], lambda kc: ada(0, kc, 0), zT2, ti * 128, BzT2)
                ropes.append(load_rope(t0 + ti * 128))
            sstats = [(sst_t[i_], Bsst[i_]) for i_ in range(4)]
            unit = 0
            units = []
            cur = {}
            for ci, (c0, ncol) in enumerate(q_tiles):
                for ti in range(4):
                    def mm(ci=ci, ti=ti):
                        if ti == 0:
                            wt, Bw = ws2.next()
                            cur["w"] = (wt.rearrange("p (k n) -> p k n", k=16), Bw)
                        wv, Bw = cur["w"]
                        pt, Bp = next_pf((0, 1, 2, 3))
                        for kc in range(16):
                            P.op("tensor", lambda h, pt=pt, wv=wv, kc=kc, ti=ti: h.matmul(
                                pt[:, 0:128], lhsT=zT2[:, kc, ti * 128:(ti + 1) * 128], rhs=wv[:, kc, :],
                                start=(kc == 0), stop=(kc == 15)), reads=[BzT2, Bw], writes=[Bp])
                        if ti == 3:
                            ws2.done()
                        return pt, Bp

                    def post(pt, Bp, ci=ci, ti=ti, ropes=ropes, sstats=sstats):
                        u2 = cnt["unit"] % 2
                        cnt["unit"] += 1
                        unit = cnt["unit"]
                        if ci < 6:
                            stt, Bs = sstats[ti]
                            P.op("scalar", lambda h, pt=pt, stt=stt, ci=ci: h.activation(
                                out=junk[:, 0:128], in_=pt[:, 0:128], func=AF.Square, accum_out=stt[:, ci:ci + 1]),
                                reads=[Bp], writes=[Bjunk, Bs])
                            P.op("vector", lambda h, pt=pt, ti=ti, ci=ci: h.tensor_tensor(
                                out=cqg[:, ti, ci * 128:(ci + 1) * 128], in0=pt[:, 0:128], in1=gq_b[:, ci * 128:(ci + 1) * 128], op=ALU.mult),
                                reads=[Bp, Bgq], writes=[Bcqg[ti]])
                            transpose_to(cqg[:, ti, ci * 128:(ci + 1) * 128], Bcqg[ti], 128, 128, cqT[:, ci, ti * 128:(ti + 1) * 128], BcqT, unit)
                            if ci == 5:
                                P.op("scalar", lambda h, stt=stt: h.activation(out=stt[:, 9:15], in_=stt[:, 0:6], func=AF.Identity, accum_out=stt[:, 6:7]),
                                     reads=[Bs], writes=[Bs])
                                rstd_from_ss(stt, Bs, 128, 768, c_ss=6, c_tmp=7, c_out=8)
                        else:
                            hq = ci - 6
                            st2, Bs2 = stat()
                            P.op("scalar", lambda h, pt=pt, st2=st2: h.activation(
                                out=junk[:, 0:128], in_=pt[:, 0:128], func=AF.Square, accum_out=st2[:, 0:1]),
                                reads=[Bp], writes=[Bjunk, Bs2])
                            rstd_from_ss(st2, Bs2, 128, 128)
                            P.op("vector", lambda h, pt=pt, u2=u2: h.tensor_tensor(out=rawq[u2][:], in0=pt[:, 0:128], in1=gqq_b[:], op=ALU.mult),
                                 reads=[Bp, Bgqq], writes=[Brawq[u2]])
                            rp = ropes[ti]
                            rope(rawq[u2][:], Brawq[u2], 128, 32, rp["BC"], rp["BS"], rp["B"],
                                 rq_t[u2][:], rq_u[u2][:], Brq_tmp[u2], rq_o[u2][:], Brq_o[u2])
                            P.op("scalar", lambda h, u2=u2, st2=st2: h.activation(out=qtok[u2][:], in_=rq_o[u2][:], func=AF.Identity, scale=st2[:, 8:9]),
                                 reads=[Brq_o[u2], Bs2], writes=[Bqtok[u2]])
                            transpose_to(qtok[u2][:], Bqtok[u2], 128, 128, qbT[:, hq, ti * 128:(ti + 1) * 128], BqbT, unit)
                    units.append((mm, post))
            run_pipeline(units)
            units = []
            for ti in range(4):
                for c3 in range(4):
                    def mm(ti=ti, c3=c3):
                        wq3 = [ws2.next(), ws2.next()]
                        pt, Bp = next_pf((0, 1, 2, 3))
                        for kc in range(6):
                            wt, Bw = wq3[kc // 3]
                            wv = wt.rearrange("p (k n) -> p k n", k=3)
                            P.op("tensor", lambda h, pt=pt, wv=wv, kc=kc, ti=ti: h.matmul(
                                pt[:, 0:384], lhsT=cqT[:, kc, ti * 128:(ti + 1) * 128], rhs=wv[:, kc % 3, :],
                                start=(kc == 0), stop=(kc == 5)), reads=[BcqT, Bw], writes=[Bp])
                        ws2.done()
                        return pt, Bp

                    def post(pt, Bp, ti=ti, c3=c3, sstats=sstats, t0=t0):
                        stt, Bs = sstats[ti]
                        qi = ti % 2
                        ptv = pt[:, 0:384].rearrange("p (h d) -> p h d", h=2)
                        P.op("scalar", lambda h, ptv=ptv, qi=qi, c3=c3, stt=stt: h.activation(
                            out=qatok[qi][:, c3 * 256:(c3 + 1) * 256].rearrange("p (h d) -> p h d", h=2), in_=ptv[:, :, 0:128],
                            func=AF.Identity, scale=stt[:, 8:9]), reads=[Bp, Bs], writes=[Bqatok[qi]])
                        P.op("scalar", lambda h, ptv=ptv, qi=qi, c3=c3, stt=stt: h.activation(
                            out=qpe[qi][:, c3 * 128:(c3 + 1) * 128].rearrange("p (h d) -> p h d", h=2), in_=ptv[:, :, 128:192],
                            func=AF.Identity, scale=stt[:, 8:9]), reads=[Bp, Bs], writes=[Bqpe[qi]])
                        if c3 == 3:
                            rp = load_ropeA(t0 + ti * 128)
                            rope(qpe[qi][:], Bqpe[qi], 512, 16, rp["AC"], rp["AS"], rp["B"],
                                 qpe_t[qi][:], qpe_u[qi][:], Bqpe_tmp[qi], qpe_o[qi][:], Bqpe_o[qi])
                            P.op("gpsimd", lambda h, qi=qi: h.tensor_copy(out=qpb[qi][:], in_=qpe_o[qi][:]), reads=[Bqpe_o[qi]], writes=[Bqpb[qi]])
                            for hh in range(8):
                                transpose_to(qatok[qi][:, hh * 128:(hh + 1) * 128], Bqatok[qi], 128, 128, qaT[:, hh, ti * 128:(ti + 1) * 128], BqaT, hh)
                            for g4 in range(4):
                                half = g4 % 2
                                P.op("tensor", lambda h, qi=qi, g4=g4, half=half: h.transpose(
                                    out=pb[half][0:64, 0:128], in_=qpb[qi][:, g4 * 64:(g4 + 1) * 64], identity=identb[:]),
                                    reads=[Bqpb[qi], Bidb], writes=[Bpb[half]])
                                P.op("tensor", lambda h, qi=qi, g4=g4, half=half: h.transpose(
                                    out=pb[half][64:128, 0:128], in_=qpb[qi][:, (g4 + 4) * 64:(g4 + 5) * 64], identity=identb[:]),
                                    reads=[Bqpb[qi], Bidb], writes=[Bpb[half]])
                                P.op("vector", lambda h, g4=g4, half=half, ti=ti: h.tensor_copy(
                                    out=qpT[:, g4, ti * 128:(ti + 1) * 128], in_=pb[half][:, 0:128]), reads=[Bpb[half]], writes=[BqpT])
                    units.append((mm, post))
            run_pipeline(units)
            if dbg and qb == 0:
                dbg_out("qaT", qaT[:, 0, :], [128, TB], BqaT, BF16)
                dbg_out("qpT", qpT[:, 0, :], [128, TB], BqpT, BF16)
                dbg_out("qbT", qbT[:, 0, :], [128, TB], BqbT, BF16)
            LOOK = 2
            pb32 = [pb[0][:, :].bitcast(F32), pb[1][:, :].bitcast(F32)]
            acc_sets = [((pf[4][:, :], Bpf[4]), (pf[5][:, :], Bpf[5])), ((pb32[0], Bpb[0]), (pb32[1], Bpb[1]))]
            steps = [(hidx, kt) for hidx in range(16) for kt in range(NKT)]
            hinfo = {}
            hcount = 0
            for hidx in range(16):
                isA = hidx < 8
                hh = hidx if isA else hidx - 8
                sh = hh if isA else 8 + hh // 4
                need_load = isA or (hh % 4 == 0)
                if need_load:
                    ki = hcount % 2
                    hcount += 1
                hinfo[hidx] = (isA, hh, sh, need_load, ki)
            Sq = {}

            def emit_S(n):
                hidx, kt = steps[n]
                isA, hh, sh, need_load, ki = hinfo[hidx]
                if kt == 0 and need_load:
                    P.dma("sync", lambda h, ki=ki, sh=sh: h.dma_start(out=kst[ki][:], in_=kas_d[sh]), writes=[Bkst[ki]])
                    P.dma("sync", lambda h, ki=ki, sh=sh: h.dma_start(out=vst[ki][:].rearrange("p k d -> p (k d)"), in_=vas_d[sh]), writes=[Bvst[ki]])
                pS, BpS = next_pf((0, 1, 2, 3))
                if isA:
                    P.op("tensor", lambda h, pS=pS, ki=ki, kt=kt, hh=hh: h.matmul(
                        pS[:, :], lhsT=kst[ki][:, kt * 128:(kt + 1) * 128], rhs=qaT[:, hh, :], start=True, stop=False),
                        reads=[Bkst[ki], BqaT], writes=[BpS])
                    pl = 0 if hh < 4 else 64
                    P.op("tensor", lambda h, pS=pS, kt=kt, hh=hh, pl=pl: h.matmul(
                        pS[:, :], lhsT=kpeT[pl:pl + 64, kt * 128:(kt + 1) * 128], rhs=qpT[pl:pl + 64, hh % 4, :], start=False, stop=True),
                        reads=[Bkpe, BqpT], writes=[BpS])
                else:
                    P.op("tensor", lambda h, pS=pS, ki=ki, kt=kt, hh=hh: h.matmul(
                        pS[:, :], lhsT=kst[ki][:, kt * 128:(kt + 1) * 128], rhs=qbT[:, hh, :], start=True, stop=True),
                        reads=[Bkst[ki], BqbT], writes=[BpS])
                Sq[n] = (pS, BpS)

            for n in range(min(LOOK, len(steps))):
                emit_S(n)
            for n in range(len(steps)):
                if n + LOOK < len(steps):
                    emit_S(n + LOOK)
                hidx, kt = steps[n]
                isA, hh, sh, need_load, ki = hinfo[hidx]
                (po, Bpo), (pd, Bpd) = acc_sets[hidx % 2]
                pS, BpS = Sq.pop(n)
                pi = n % 4
                P.op("scalar", lambda h, pS=pS, pi=pi, isA=isA: h.activation(
                    out=pT[pi][:], in_=pS[:, :], func=AF.Exp, scale=(SCA if isA else SCB)), reads=[BpS], writes=[BpT[pi]])
                P.op("tensor", lambda h, po=po, ki=ki, kt=kt, pi=pi: h.matmul(
                    po, lhsT=vst[ki][:, kt, :], rhs=pT[pi][:], start=(kt == 0), stop=(kt == NKT - 1)),
                    reads=[Bvst[ki], BpT[pi]], writes=[Bpo])
                P.op("tensor", lambda h, pd=pd, kt=kt, pi=pi: h.matmul(
                    pd, lhsT=onesb[:], rhs=pT[pi][:], start=(kt == 0), stop=(kt == NKT - 1)),
                    reads=[Bones, BpT[pi]], writes=[Bpd])
                if kt == NKT - 1:
                    ri = hidx % 2
                    P.op("scalar", lambda h, pd=pd, ri=ri: h.activation(out=rden[ri][:], in_=pd, func=AF.Ln), reads=[Bpd], writes=[Brden[ri]])
                    P.op("scalar", lambda h, ri=ri: h.activation(out=rden[ri][:], in_=rden[ri][:], func=AF.Exp, scale=-1.0), reads=[Brden[ri]], writes=[Brden[ri]])
                    P.op("vector", lambda h, po=po, ri=ri, hidx=hidx: h.tensor_tensor(out=oT[:, hidx, :], in0=po, in1=rden[ri][:], op=ALU.mult),
                         reads=[Bpo, Brden[ri]], writes=[BoT])
            if dbg and qb == 0:
                dbg_out("oTa", oT[:, 0, :], [128, TB], BoT, BF16)
                dbg_out("oTb", oT[:, 8, :], [128, TB], BoT, BF16)
            P.barrier()
            for j in range(16):
                wa, Bwa = ws2.next()
                wav = wa.rearrange("p (k n) -> p k n", k=16)
                wga, Bwga = ws2.next()
                wgav = wga.rearrange("p (k n) -> p k n", k=16)
                wgb, Bwgb = ws2.next()
                wgbv = wgb.rearrange("p (k n) -> p k n", k=16)
                pa, Bpa = next_pf((0, 1, 2, 3))
                for k in range(8):
                    P.op("tensor", lambda h, pa=pa, wav=wav, k=k: h.matmul(pa[:, :], lhsT=wav[:, k, :], rhs=oT[:, k, :], start=(k == 0), stop=(k == 7)),
                         reads=[Bwa, BoT], writes=[Bpa])
                pbr, Bpbr = next_pf((0, 1, 2, 3))
                for k in range(8):
                    P.op("tensor", lambda h, pbr=pbr, wav=wav, k=k: h.matmul(pbr[:, :], lhsT=wav[:, 8 + k, :], rhs=oT[:, 8 + k, :], start=(k == 0), stop=(k == 7)),
                         reads=[Bwa, BoT], writes=[Bpbr])
                pga, Bpga = next_pf((0, 1, 2, 3))
                for kc in range(16):
                    P.op("tensor", lambda h, pga=pga, wgav=wgav, kc=kc: h.matmul(pga[:, :], lhsT=wgav[:, kc, :], rhs=zT2[:, kc, :], start=(kc == 0), stop=(kc == 15)),
                         reads=[Bwga, BzT2], writes=[Bpga])
                pgb, Bpgb = next_pf((0, 1, 2, 3))
                for kc in range(16):
                    P.op("tensor", lambda h, pgb=pgb, wgbv=wgbv, kc=kc: h.matmul(pgb[:, :], lhsT=wgbv[:, kc, :], rhs=zT2[:, kc, :], start=(kc == 0), stop=(kc == 15)),
                         reads=[Bwgb, BzT2], writes=[Bpgb])
                ws2.done()
                P.op("scalar", lambda h, pga=pga: h.activation(out=gsig[0][:], in_=pga[:, :], func=AF.Sigmoid), reads=[Bpga], writes=[Bgsig[0]])
                P.op("scalar", lambda h, pgb=pgb: h.activation(out=gsig[1][:], in_=pgb[:, :], func=AF.Sigmoid), reads=[Bpgb], writes=[Bgsig[1]])
                P.op("vector", lambda h, pa=pa: h.tensor_tensor(out=mtmp[0][:], in0=pa[:, :], in1=gsig[0][:], op=ALU.mult), reads=[Bpa, Bgsig[0]], writes=[Bmtmp[0]])
                P.op("vector", lambda h, pbr=pbr: h.tensor_tensor(out=mtmp[1][:], in0=pbr[:, :], in1=gsig[1][:], op=ALU.mult), reads=[Bpbr, Bgsig[1]], writes=[Bmtmp[1]])
                P.op("gpsimd", lambda h, j=j: h.tensor_tensor(out=mT[:, j, :], in0=mtmp[0][:], in1=mtmp[1][:], op=ALU.add), reads=[Bmtmp[0], Bmtmp[1]], writes=[BmT])
            if dbg and qb == 0:
                dbg_out("mT", mT[:, 0, :], [128, TB], BmT, BF16)
            P.barrier()
            units = []
            for jo in range(16):
                def mm(jo=jo):
                    wo, Bwo = ws2.next()
                    wov = wo.rearrange("p (k n) -> p k n", k=16)
                    pt, Bp = next_pf((0, 1, 2, 3))
                    for kc in range(16):
                        P.op("tensor", lambda h, pt=pt, wov=wov, kc=kc: h.matmul(pt[:, :], lhsT=wov[:, kc, :], rhs=mT[:, kc, :], start=(kc == 0), stop=(kc == 15)),
                             reads=[Bwo, BmT], writes=[Bp])
                    ws2.done()
                    return pt, Bp

                def post(pt, Bp, jo=jo, t0=t0):
                    ai = jo % 2
                    P.op("scalar", lambda h, pt=pt, ai=ai, jo=jo: h.activation(out=aoT2[ai][:], in_=pt[:, :], func=AF.Identity, scale=ada(2, jo, 0)),
                         reads=[Bp, Bada], writes=[BaoT2[ai]])
                    P.dma("sync", lambda h, ai=ai, jo=jo, t0=t0: h.dma_start(
                        out=xres[ai][:], in_=x_d[t0:t0 + TB, jo * 128:(jo + 1) * 128].rearrange("(t p) d -> p t d", p=128)), writes=[Bxres[ai]])
                    p2, Bp2 = pf[4 + ai], Bpf[4 + ai]
                    for ti in range(4):
                        P.op("tensor", lambda h, p2=p2, ai=ai, ti=ti: h.transpose(out=p2[:, ti * 128:(ti + 1) * 128], in_=aoT2[ai][:, ti * 128:(ti + 1) * 128], identity=identf[:]),
                             reads=[BaoT2[ai], Bidf], writes=[Bp2])
                    P.op("vector", lambda h, p2=p2, ai=ai: h.tensor_tensor(out=xres[ai][:], in0=p2[:, :].rearrange("p (t d) -> p t d", t=4), in1=xres[ai][:], op=ALU.add),
                         reads=[Bp2, Bxres[ai]], writes=[Bxres[ai]])
                    P.dma("scalar", lambda h, ai=ai, jo=jo, t0=t0: h.dma_start(
                        out=x1s_d[t0:t0 + TB, jo * 128:(jo + 1) * 128].rearrange("(t p) d -> p t d", p=128), in_=xres[ai][:]), reads=[Bxres[ai]], sembuf=Bxres[ai])
                units.append((mm, post))
            run_pipeline(units)

        if stop < 3:
            P.emit(final_bufs=dbg_bufs)
            return nc, dbg_outs
        P.barrier()
        AR.reset()
        cast_engs[:] = ["scalar", "vector"]
        z2T = AR.bf(16 * TB).rearrange("p (k t) -> p k t", k=16); Bz2T = P.buf("z2T")
        zh = AR.bf(16 * 8).rearrange("p (k t) -> p k t", k=16); Bzh = P.buf("zh")
        hT = AR.bf(NFC * TB).rearrange("p (f t) -> p f t", f=NFC); BhT = P.buf("hT")
        x2 = AR.f32(4 * D).rearrange("p (t d) -> p t d", t=4); Bx2 = P.bufs("x2", 4)
        uh = AR.f32(88 * 8).rearrange("p (j c) -> p j c", j=88); Buh = P.buf("uh")
        gfin = AR.f32(D); Bgfin = P.buf("gfin")
        ca = [AR.f32(TB) for _ in range(2)]; Bca = P.bufs("ca", 2)
        cb = [AR.f32(TB) for _ in range(2)]; Bcb = P.bufs("cb", 2)
        sa = [AR.f32(TB) for _ in range(2)]; Bsa = P.bufs("sa", 2)
        aoT = [AR.f32(TB) for _ in range(2)]; BaoT = P.bufs("aoT3", 2)
        print("phase3 arena words", AR.off)
        P.dma("sync", lambda h: h.dma_start(out=gfin, in_=gfin_d.partition_broadcast(128)), writes=[Bgfin])
        for bi in range(3):
            P.dma("sync", lambda h, bi=bi: h.dma_start(out=x2[2 * bi:2 * bi + 2, 3, :], in_=x1s_d[512 * (bi + 1) - 1:512 * (bi + 1) + 1, :]), writes=[Bx2[3]])
        norm_tile_T(x2[0:6, 3, :], Bx2[3], 6, lambda kc: a2[:, kc:kc + 1], lambda kc: ada(3, kc, 0), zh, 0, Bzh)
        P.op("gpsimd", lambda h: h.memset(uh[:].rearrange("p j c -> p (j c)"), 0.0), writes=[Buh])
        specs3 = []
        for _blk in range(NB):
            for j in range(NFC):
                specs3.append([wrows(wup_d, 0, 8, j * 128, 128), wrows(wup_d, 0, 8, DFF + j * 128, 128)])
                specs3.append([wrows(wup_d, 8, 8, j * 128, 128), wrows(wup_d, 8, 8, DFF + j * 128, 128)])
            for jo in range(16):
                for (f0, nf) in ((0, 16), (16, 16), (32, 12)):
                    specs3.append([wrows(wdown_d, f0, nf, jo * 128, 128)])
        ws3 = WStream(specs3)
        for blk in range(NB):
            t0 = blk * TB
            for ti in range(4):
                P.dma("sync", lambda h, ti=ti, t0=t0: h.dma_start(out=x2[:, ti, :], in_=x1s_d[t0 + ti * 128:t0 + (ti + 1) * 128, :]), writes=[Bx2[ti]])
                norm_tile_T(x2[:, ti, :], Bx2[ti], 128, lambda kc: a2[:, kc:kc + 1], lambda kc: ada(3, kc, 0), z2T, ti * 128, Bz2T)
            if dbg and blk == 0:
                dbg_out("z2T", z2T[:, 0, :], [128, TB], Bz2T, BF16)
            for j in range(NFC):
                wu, Bwu = ws3.next()
                wu2, Bwu2 = ws3.next()
                wuv = wu.rearrange("p (s k n) -> p s k n", s=2, k=8)
                wu2v = wu2.rearrange("p (s k n) -> p s k n", s=2, k=8)
                pu = []
                for s in range(2):
                    pt, Bp = next_pf((0, 1, 2, 3))
                    pu.append((pt, Bp))
                    for kc in range(16):
                        wsel, Bsel = (wuv, Bwu) if kc < 8 else (wu2v, Bwu2)
                        P.op("tensor", lambda h, pt=pt, wsel=wsel, s=s, kc=kc: h.matmul(
                            pt[:, :], lhsT=wsel[:, s, kc % 8, :], rhs=z2T[:, kc, :], start=(kc == 0), stop=(kc == 15)),
                            reads=[Bsel, Bz2T], writes=[Bp])
                    if blk == 0:
                        ph, Bph = pf[4 + s], Bpf[4 + s]
                        for kc in range(16):
                            wsel, Bsel = (wuv, Bwu) if kc < 8 else (wu2v, Bwu2)
                            P.op("tensor", lambda h, ph=ph, wsel=wsel, s=s, kc=kc: h.matmul(
                                ph[:, 0:6], lhsT=wsel[:, s, kc % 8, :], rhs=zh[:, kc, 0:6], start=(kc == 0), stop=(kc == 15)),
                                reads=[Bsel, Bzh], writes=[Bph])
                        P.op("vector", lambda h, ph=ph, s=s, j=j: h.tensor_copy(out=uh[:, s * NFC + j, 1:7], in_=ph[:, 0:6]), reads=[Bph], writes=[Buh])
                ws3.done()
                ci2 = j % 2
                for s in range(2):
                    pt, Bp = pu[s]
                    cdst, Bc = (ca[ci2], Bca[ci2]) if s == 0 else (cb[ci2], Bcb[ci2])
                    fj = s * NFC + j
                    P.op("scalar", lambda h, pt=pt, cdst=cdst, fj=fj: h.activation(out=cdst[:], in_=pt[:, :], func=AF.Identity,
                                                                                    scale=convp[:, fj, 1:2], bias=convp[:, fj, 3:4]),
                         reads=[Bp, Bconv], writes=[Bc])
                    P.op("vector", lambda h, pt=pt, cdst=cdst, fj=fj: h.scalar_tensor_tensor(
                        out=cdst[:, 1:TB], in0=pt[:, 0:TB - 1], scalar=convp[:, fj, 0:1], in1=cdst[:, 1:TB], op0=ALU.mult, op1=ALU.add),
                        reads=[Bp, Bconv, Bc], writes=[Bc])
                    P.op("vector", lambda h, pt=pt, cdst=cdst, fj=fj: h.scalar_tensor_tensor(
                        out=cdst[:, 0:TB - 1], in0=pt[:, 1:TB], scalar=convp[:, fj, 2:3], in1=cdst[:, 0:TB - 1], op0=ALU.mult, op1=ALU.add),
                        reads=[Bp, Bconv, Bc], writes=[Bc])
                    lcol = 2 * blk - 1 if blk > 0 else 0
                    rcol = 2 * blk + 2 if blk < NB - 1 else 7
                    P.op("vector", lambda h, cdst=cdst, fj=fj, lcol=lcol: h.scalar_tensor_tensor(
                        out=cdst[:, 0:1], in0=uh[:, fj, lcol:lcol + 1], scalar=convp[:, fj, 0:1], in1=cdst[:, 0:1], op0=ALU.mult, op1=ALU.add),
                        reads=[Buh, Bconv, Bc], writes=[Bc])
                    P.op("vector", lambda h, cdst=cdst, fj=fj, rcol=rcol: h.scalar_tensor_tensor(
                        out=cdst[:, TB - 1:TB], in0=uh[:, fj, rcol:rcol + 1], scalar=convp[:, fj, 2:3], in1=cdst[:, TB - 1:TB], op0=ALU.mult, op1=ALU.add),
                        reads=[Buh, Bconv, Bc], writes=[Bc])
                P.op("scalar", lambda h, ci2=ci2: h.activation(out=sa[ci2][:], in_=ca[ci2][:], func=AF.Silu), reads=[Bca[ci2]], writes=[Bsa[ci2]])
                P.op("gpsimd", lambda h, ci2=ci2, j=j: h.tensor_tensor(out=hT[:, j, :], in0=sa[ci2][:], in1=cb[ci2][:], op=ALU.mult),
                     reads=[Bsa[ci2], Bcb[ci2]], writes=[BhT])
            if dbg and blk == 0:
                dbg_out("hT", hT[:, 0, :], [128, TB], BhT, BF16)
            units = []
            for jo in range(16):
                def mm(jo=jo):
                    wparts = []
                    for (f0, nf) in ((0, 16), (16, 16), (32, 12)):
                        wparts.append((ws3.next(), f0, nf))
                    pt, Bp = next_pf((0, 1, 2, 3))
                    for (wd, Bwd), f0, nf in wparts:
                        wdv = wd.rearrange("p (k n) -> p k n", k=nf)
                        for f in range(nf):
                            P.op("tensor", lambda h, pt=pt, wdv=wdv, f=f, f0=f0: h.matmul(
                                pt[:, :], lhsT=wdv[:, f, :], rhs=hT[:, f0 + f, :], start=(f0 + f == 0), stop=(f0 + f == NFC - 1)),
                                reads=[Bwd, BhT], writes=[Bp])
                    ws3.done()
                    return pt, Bp

                def post(pt, Bp, jo=jo):
                    ai = jo % 2
                    P.op("scalar", lambda h, pt=pt, ai=ai, jo=jo: h.activation(out=aoT[ai][:], in_=pt[:, :], func=AF.Identity, scale=ada(5, jo, 0)),
                         reads=[Bp, Bada], writes=[BaoT[ai]])
                    p2, Bp2 = pf[4 + ai], Bpf[4 + ai]
                    for ti in range(4):
                        P.op("tensor", lambda h, p2=p2, ai=ai, ti=ti: h.transpose(out=p2[:, ti * 128:(ti + 1) * 128], in_=aoT[ai][:, ti * 128:(ti + 1) * 128], identity=identf[:]),
                             reads=[BaoT[ai], Bidf], writes=[Bp2])
                    P.op("vector", lambda h, p2=p2, jo=jo: h.tensor_tensor(
                        out=x2[:, :, jo * 128:(jo + 1) * 128], in0=p2[:, :].rearrange("p (t d) -> p t d", t=4), in1=x2[:, :, jo * 128:(jo + 1) * 128], op=ALU.add),
                        reads=[Bp2] + Bx2, writes=Bx2)
                units.append((mm, post))
            run_pipeline(units)
            for ti in range(4):
                stt, Bs = stat()
                P.op("scalar", lambda h, ti=ti, stt=stt: h.activation(out=junk[:, :], in_=x2[:, ti, :], func=AF.Square, accum_out=stt[:, 0:1]),
                     reads=[Bx2[ti]], writes=[Bjunk, Bs])
                rstd_from_ss(stt, Bs, 128, D)
                P.op("vector", lambda h, ti=ti, stt=stt: h.scalar_tensor_tensor(
                    out=x2[:, ti, :], in0=x2[:, ti, :], scalar=stt[:, 8:9], in1=gfin, op0=ALU.mult, op1=ALU.mult),
                    reads=[Bx2[ti], Bs, Bgfin], writes=[Bx2[ti]])
                P.dma("scalar", lambda h, ti=ti, t0=t0: h.dma_start(out=out_d[t0 + ti * 128:t0 + (ti + 1) * 128, :], in_=x2[:, ti, :]),
                      reads=[Bx2[ti]], sembuf=Bx2[ti])

        P.emit(final_bufs=list(Bx2) + dbg_bufs)
        print("nsem", P.nsem, "nwaits", P.nwaits, {e: len(P.recs[e]) for e in P.ENGS})
    return nc, dbg_outs


def _rope_tables(rot_dim, reps):
    n_rows = T // GRID_W
    row = np.repeat(np.arange(n_rows, dtype=np.float32), GRID_W)
    col = np.tile(np.arange(GRID_W, dtype=np.float32), n_rows)
    half = rot_dim // 2
    q = rot_dim // 4
    inv_freq = (np.float32(10000.0) ** (-np.arange(0, half, 2, dtype=np.float32) / np.float32(half))).astype(np.float32)
    ang = np.concatenate([row[:, None] * inv_freq, col[:, None] * inv_freq], axis=-1).astype(np.float32)
    c = np.cos(ang).astype(np.float32).reshape(T, 2, q)
    s = np.sin(ang).astype(np.float32).reshape(T, 2, q)
    C = np.stack([c, c], axis=2).reshape(T, rot_dim)
    S = np.stack([-s, s], axis=2).reshape(T, rot_dim)
    return np.ascontiguousarray(np.tile(C, (1, reps))), np.ascontiguousarray(np.tile(S, (1, reps)))


_CACHE = {}


def kernel(x, c, ctx, c_ctx, w_ada, b_ada, norm1_g, w_in, mla_q_norm_g, w_q_up, mla_kv_norm_g,
           w_kv_up, gqa_q_norm_g, gqa_k_norm_g, w_br_a, w_br_b, w_out, norm2_g, w_up, conv_w,
           conv_b, w_down, final_norm_g, _dbg=False):
    f = lambda a: np.ascontiguousarray(np.asarray(a, dtype=np.float32))
    x, c, ctx, c_ctx = f(x), f(c), f(ctx), f(c_ctx)
    import os
    stop = float(os.environ.get("K_STOP", "99")) if _dbg else 99
    ncores = int(os.environ.get("K_CORES", "8")) if _dbg else 8
    key = ("dbg" if _dbg else "prod", stop)
    if key not in _CACHE:
        _CACHE[key] = build_program(dbg=_dbg, stop=stop)
    nc, dbg_outs = _CACHE[key]
    AC, AS = _rope_tables(64, 8)
    BC, BS = _rope_tables(128, 1)

    def fm(v, n):
        return np.ascontiguousarray(f(v).reshape(n, 128).T)
    convp = np.stack([f(conv_w)[0, 0], f(conv_w)[0, 1], f(conv_w)[0, 2], f(conv_b)[0]], axis=-1)
    convp = np.ascontiguousarray(convp.reshape(88, 128, 4).transpose(1, 0, 2).reshape(128, 88 * 4))
    shared = {
        "w_ada": f(w_ada)[0], "badaT": fm(b_ada[0], 96), "n1g": fm(norm1_g[0], 16), "n2g": fm(norm2_g[0], 16),
        "w_in": f(w_in)[0], "gq": f(mla_q_norm_g)[0], "w_q_up": f(w_q_up)[0], "gkv": f(mla_kv_norm_g)[0],
        "w_kv_up": f(w_kv_up)[0], "gqq": f(gqa_q_norm_g)[0], "gqk": f(gqa_k_norm_g)[0],
        "w_br_a": f(w_br_a)[0], "w_br_b": f(w_br_b)[0], "w_out": f(w_out)[0], "w_up": f(w_up)[0],
        "convp": convp, "w_down": f(w_down)[0], "gfin": f(final_norm_g),
        "identf": np.eye(128, dtype=np.float32), "ropeAC": AC, "ropeAS": AS, "ropeBC": BC, "ropeBS": BS,
    }
    in_maps = []
    for b in range(ncores):
        cv = np.stack([c[b].reshape(16, 128).T, c_ctx.reshape(16, 128).T], axis=-1)
        m = dict(shared)
        m["x"] = x[b]
        m["ctx"] = ctx[b]
        m["cvec"] = np.ascontiguousarray(cv.reshape(128, 32))
        in_maps.append(m)
    res = run_bass_kernel_spmd(nc, in_maps, core_ids=list(range(ncores)))
    out = np.stack([np.asarray(r["out"], dtype=np.float32) for r in res.results], axis=0)
    if ncores < 8:
        out = np.concatenate([out, np.zeros((8 - ncores, T, D), np.float32)], 0)
    if _dbg:
        DEBUG.clear()
        for k in dbg_outs:
            DEBUG[k] = np.asarray(res.results[0]["dbg_" + k])
    return out
```

```python
import numpy as np
from contextlib import ExitStack
import concourse.bass as bass
import concourse.mybir as mybir
from concourse.bass_utils import run_bass_kernel_spmd

F32 = mybir.dt.float32
BF16 = mybir.dt.bfloat16
AF = mybir.ActivationFunctionType
ALU = mybir.AluOpType

D = 2048
T = 2048
CTX = 256
TK = T + CTX
NKT = TK // 128
GRID_W = 64
EPS = 1e-6
KV_COLS = 1088
Q_COLS = 1792
IN_COLS = 6976
DFF = 5632
NFC = DFF // 128
TB = 512
NB = T // TB
DEBUG = {}


class Buf:
    __slots__ = ("name", "last_write", "readers", "dsem", "dcount", "excl")

    def __init__(self, name):
        self.name = name
        self.excl = False
        self.last_write = None
        self.readers = {}
        self.dsem = None
        self.dcount = 0


class Rec:
    __slots__ = ("eng", "fn", "deps", "signal", "sigval", "kind", "dsem", "dval")

    def __init__(self, eng, fn, deps, kind):
        self.eng = eng
        self.fn = fn
        self.deps = deps
        self.signal = False
        self.sigval = None
        self.kind = kind
        self.dsem = None
        self.dval = None


class Prog:
    ENGS = ("tensor", "vector", "scalar", "gpsimd", "sync")

    def __init__(self, nc, stack):
        self.nc = nc
        self.stack = stack
        self.recs = {e: [] for e in self.ENGS}
        self.esem = {e: stack.enter_context(nc.semaphore("es_" + e)) for e in self.ENGS}
        self.nsem = 0
        self.pending_barrier = {e: [] for e in self.ENGS}
        self.dmas_since_barrier = []

    def buf(self, name):
        return Buf(name)

    def bufs(self, name, n):
        return [Buf(f"{name}{i}") for i in range(n)]

    def _collect(self, reads, writes, sem_skip=None, eng=None):
        deps = []
        for b in reads:
            if b.last_write is not None:
                deps.append(b.last_write)
            if b.excl:
                for e2, t in b.readers.items():
                    if e2 != eng:
                        deps.append(t)
        for b in writes:
            lw = b.last_write
            if lw is not None and not (sem_skip is not None and lw.kind == "d" and lw.dsem is sem_skip):
                deps.append(lw)
            for t in b.readers.values():
                deps.append(t)
        return deps

    def op(self, eng, fn, reads=(), writes=()):
        deps = self._collect(reads, writes, eng=eng)
        if self.pending_barrier[eng]:
            deps.extend(self.pending_barrier[eng])
            self.pending_barrier[eng] = []
        r = Rec(eng, fn, deps, "c")
        self.recs[eng].append(r)
        for b in reads:
            b.readers[eng] = r
        for b in writes:
            b.last_write = r
            b.readers = {}
        return r

    def dma(self, eng, fn, reads=(), writes=(), sembuf=None):
        sb = sembuf or (writes[0] if writes else reads[0])
        if sb.dsem is None:
            sb.dsem = self.stack.enter_context(self.nc.semaphore(f"ds{self.nsem}"))
            self.nsem += 1
        deps = self._collect(reads, writes, sem_skip=sb.dsem)
        if self.pending_barrier[eng]:
            deps.extend(self.pending_barrier[eng])
            self.pending_barrier[eng] = []
        r = Rec(eng, fn, deps, "d")
        sb.dcount += 1
        r.dsem = sb.dsem
        r.dval = 16 * sb.dcount
        self.recs[eng].append(r)
        self.dmas_since_barrier.append(r)
        for b in reads:
            b.readers["dma:" + str(id(sb))] = r
        for b in writes:
            b.last_write = r
            b.readers = {}
        return r

    def barrier(self):
        toks = []
        for e in self.ENGS:
            for r in reversed(self.recs[e]):
                if r.kind == "c":
                    toks.append(r)
                    break
        latest = {}
        for r in self.dmas_since_barrier:
            latest[id(r.dsem)] = r
        toks.extend(latest.values())
        self.dmas_since_barrier = []
        for e in self.ENGS:
            self.pending_barrier[e] = list(toks)

    def emit(self, final_bufs=()):
        nc = self.nc
        same_ok = ("tensor",)
        for e in self.ENGS:
            for r in self.recs[e]:
                for d in r.deps:
                    if d.kind == "c":
                        if d.eng == r.eng and r.kind == "c" and d.eng in same_ok:
                            continue
                        d.signal = True
        for e in self.ENGS:
            c = 0
            for r in self.recs[e]:
                if r.kind == "c" and r.signal:
                    c += 1
                    r.sigval = c
        self.nwaits = 0
        all_sems = [self.esem[e] for e in self.ENGS]
        seen = set()
        for e in self.ENGS:
            for r in self.recs[e]:
                if r.kind == "d" and id(r.dsem) not in seen:
                    seen.add(id(r.dsem))
                    all_sems.append(r.dsem)
        for bf in final_bufs:
            if bf.dsem is not None and id(bf.dsem) not in seen:
                seen.add(id(bf.dsem))
                all_sems.append(bf.dsem)
        with nc.Block() as block0:
            def clr(h):
                for sm in all_sems:
                    h.sem_clear(sm)
            block0.gpsimd(clr)
        with nc.Block() as block:
            for e in self.ENGS:
                def body(h, e=e):
                    waited = {}
                    for r in self.recs[e]:
                        need = {}
                        for d in r.deps:
                            if d.kind == "c":
                                if d.eng == r.eng and r.kind == "c" and d.eng in same_ok:
                                    continue
                                key = ("c", d.eng)
                                sem, val = self.esem[d.eng], d.sigval
                            else:
                                key = ("d", id(d.dsem))
                                sem, val = d.dsem, d.dval
                            if waited.get(key, 0) >= val:
                                continue
                            if key not in need or need[key][1] < val:
                                need[key] = (sem, val)
                        for key, (sem, val) in need.items():
                            h.wait_ge(sem, val)
                            waited[key] = val
                            self.nwaits += 1
                        inst = r.fn(h)
                        if r.kind == "d":
                            inst.then_inc(r.dsem, 16)
                        elif r.signal:
                            inst.then_inc(self.esem[e], 1)
                    if e == "sync":
                        for b in final_bufs:
                            h.wait_ge(b.dsem, 16 * b.dcount)
                getattr(block, e)(body)


class _Stop(Exception):
    pass


def build_program(dbg=False, stop=99):
    nc = bass.Bass("TRN2", target_bir_lowering=False)
    try:
        return _build(nc, dbg, stop)
    except _Stop as e:
        return nc, e.args[0]


def _build(nc, dbg, stop):

    def din(name, shape, dt=F32):
        return nc.dram_tensor(name, list(shape), dt, kind="ExternalInput").ap()

    x_d = din("x", [T, D])
    ctx_d = din("ctx", [CTX, D])
    cvec_d = din("cvec", [128, 32])
    wada_d = din("w_ada", [D, 6 * D])
    bada_d = din("badaT", [128, 96])
    n1g_d = din("n1g", [128, 16])
    n2g_d = din("n2g", [128, 16])
    win_d = din("w_in", [D, IN_COLS])
    gq_d = din("gq", [768])
    wqup_d = din("w_q_up", [768, 1536])
    gkv_d = din("gkv", [512])
    wkvup_d = din("w_kv_up", [512, 2048])
    gqq_d = din("gqq", [128])
    gqk_d = din("gqk", [128])
    wbra_d = din("w_br_a", [1024, D])
    wbrb_d = din("w_br_b", [1024, D])
    wout_d = din("w_out", [D, D])
    wup_d = din("w_up", [D, 2 * DFF])
    convp_d = din("convp", [128, 88 * 4])
    wdown_d = din("w_down", [DFF, D])
    gfin_d = din("gfin", [D])
    identf_d = din("identf", [128, 128])
    ropeAC_d = din("ropeAC", [T, 512])
    ropeAS_d = din("ropeAS", [T, 512])
    ropeBC_d = din("ropeBC", [T, 128])
    ropeBS_d = din("ropeBS", [T, 128])
    out_d = nc.dram_tensor("out", [T, D], F32, kind="ExternalOutput").ap()
    x1s_d = nc.dram_tensor("x1s", [T, D], F32, kind="Internal").ap()
    kas_d = nc.dram_tensor("kas", [10, 128, TK], BF16, kind="Internal").ap()
    vas_d = nc.dram_tensor("vas", [10, 128, NKT * 128], BF16, kind="Internal").ap()
    dbg_outs = {}

    with ExitStack() as st:
        P = Prog(nc, st)

        def sb(name, shape, dt):
            return st.enter_context(nc.sbuf_tensor("s_" + name, list(shape), dt))

        def ps(name, shape, dt):
            return st.enter_context(nc.psum_tensor("p_" + name, list(shape), dt))

        identf = sb("identf", [128, 128], F32); Bidf = P.buf("identf")
        identb = sb("identb", [128, 128], BF16); Bidb = P.buf("identb")
        onesb = sb("onesb", [128, 128], BF16); Bones = P.buf("onesb")
        epsc = sb("epsc", [128, 1], F32); Beps = P.buf("eps")
        cvec = sb("cvec", [128, 32], F32); Bcvec = P.buf("cvec")
        csb = sb("csb", [128, 32], BF16); Bcsb = P.buf("csb")
        adaT = sb("adaT", [128, 96, 2], F32); Bada = P.buf("adaT")
        badaT = sb("badaT", [128, 96], F32); Bbada = P.buf("badaT")
        n1g = sb("n1g", [128, 16], F32); Bn1g = P.buf("n1g")
        n2g = sb("n2g", [128, 16], F32); Bn2g = P.buf("n2g")
        a1 = sb("a1", [128, 16], F32)
        a1c = sb("a1c", [128, 16], F32)
        a2 = sb("a2", [128, 16], F32)
        Bmods = P.buf("mods")
        gq_b = sb("gq_b", [128, 768], F32); Bgq = P.buf("gq")
        gkv_b = sb("gkv_b", [128, 512], F32); Bgkv = P.buf("gkv")
        gqq_b = sb("gqq_b", [128, 128], F32); Bgqq = P.buf("gqq")
        gqk_b = sb("gqk_b", [128, 128], F32); Bgqk = P.buf("gqk")
        convp = sb("convp", [128, 88, 4], F32); Bconv = P.buf("convp")
        kpeT = sb("kpeT", [128, TK], BF16); Bkpe = P.buf("kpeT")
        junk = sb("junk", [128, 2048], BF16); Bjunk = P.buf("junk")
        ybf = [sb(f"ybf{i}", [128, 2048], BF16) for i in range(2)]; Bybf = P.bufs("ybf", 2)
        Bxst = P.bufs("xst", 2)
        xst = [None, None]
        sst_t = [sb(f"sst{i}", [128, 16], F32) for i in range(4)]; Bsst = P.bufs("sst", 4)
        NSTAT = 8
        stat_t = [sb(f"stat{i}", [128, 16], F32) for i in range(NSTAT)]; Bstat = P.bufs("stat", NSTAT)
        rtabB = [sb(f"rtabB{i}", [128, 384], F32) for i in range(4)]; BrtabB = P.bufs("rtabB", 4)
        rtabA = [sb(f"rtabA{i}", [128, 1024], F32) for i in range(1)]; BrtabA = P.bufs("rtabA", 1)
        WSLOT = 2048
        NSTG, NWB = 4, 4
        wstg = [sb(f"wstg{i}", [128, WSLOT], F32) for i in range(NSTG)]; Bwstg = P.bufs("wstg", NSTG)
        wbf = [sb(f"wbf{i}", [128, WSLOT], BF16) for i in range(NWB)]; Bwbf = P.bufs("wbf", NWB)
        ARENA_W = 30720
        arena = sb("arena", [128, ARENA_W], F32)

        pf = [ps(f"pf{i}", [128, 512], F32) for i in range(6)]; Bpf = P.bufs("pf", 6)
        pb = [ps(f"pb{i}", [128, 1024], BF16) for i in range(2)]; Bpb = P.bufs("pb", 2)
        for b_ in Bpf + Bpb:
            b_.excl = True

        cnt = {"unit": 0, "stat": 0, "w": 0, "wb": 0, "cast": 0, "evac": 0, "pf": 0, "rt": 0, "rtA": 0, "ybf": 0, "xst": 0}

        def stat():
            i = cnt["stat"] % NSTAT
            cnt["stat"] += 1
            return stat_t[i], Bstat[i]

        class Arena:
            def __init__(self):
                self.off = 0

            def reset(self):
                self.off = 0

            def f32(self, n):
                a = arena[:, self.off:self.off + n]
                self.off += n
                assert self.off <= ARENA_W, self.off
                return a

            def bf(self, n):
                w = (n + 1) // 2
                a = arena[:, self.off:self.off + w].bitcast(BF16)
                self.off += w
                assert self.off <= ARENA_W, self.off
                return a

        AR = Arena()

        def dbg_out(name, ap, shape, Bsrc, dt=F32):
            if not dbg:
                return
            d = nc.dram_tensor("dbg_" + name, list(shape), dt, kind="ExternalOutput").ap()
            dbg_outs[name] = (list(shape), dt)
            b = P.buf("dbg_" + name)
            P.dma("sync", lambda h: h.dma_start(out=d, in_=ap), reads=[Bsrc], sembuf=b)
            dbg_bufs.append(b)

        dbg_bufs = []

        def stop_at(level):
            if stop < level:
                P.emit(final_bufs=dbg_bufs)
                raise _Stop(dbg_outs)

        cast_engs = ["vector", "scalar"]

        def load_w(parts, dst=None, Bdst=None):
            i = cnt["w"]
            cnt["w"] += 1
            si = i % NSTG
            off = 0
            for ap in parts:
                shp = list(ap.shape)
                n = 1
                for s in shp[1:]:
                    n *= s
                if len(shp) == 3:
                    dstv = wstg[si][:, off:off + n].rearrange("p (a b) -> p a b", a=shp[1])
                else:
                    dstv = wstg[si][:, off:off + n]
                P.dma("sync", lambda h, dstv=dstv, ap=ap: h.dma_start(out=dstv, in_=ap), writes=[Bwstg[si]])
                off += n
            assert off <= WSLOT
            if dst is None:
                bi = cnt["wb"] % NWB
                cnt["wb"] += 1
                dst, Bdst = wbf[bi][:, 0:off], Bwbf[bi]
            eng = cast_engs[cnt["cast"] % len(cast_engs)]
            cnt["cast"] += 1
            if eng == "scalar":
                P.op(eng, lambda h, dst=dst, si=si, off=off: h.copy(out=dst, in_=wstg[si][:, 0:off]),
                     reads=[Bwstg[si]], writes=[Bdst])
            else:
                P.op(eng, lambda h, dst=dst, si=si, off=off: h.tensor_copy(out=dst, in_=wstg[si][:, 0:off]),
                     reads=[Bwstg[si]], writes=[Bdst])
            return dst, Bdst

        def wrows(w_d, k0, nk, c0, ncol):
            return w_d[k0 * 128:(k0 + nk) * 128, c0:c0 + ncol].rearrange("(k p) n -> p k n", p=128)

        class WStream:
            def __init__(self, specs):
                self.specs = specs
                self.tiles = []
                self.consumed = 0
                self.released = 0
                self.prime()

            def prime(self):
                while len(self.tiles) < len(self.specs) and len(self.tiles) - self.released < NWB:
                    self.tiles.append(load_w(self.specs[len(self.tiles)]))

            def next(self):
                assert self.consumed < len(self.tiles), "weight stream underrun (call done())"
                t = self.tiles[self.consumed]
                self.consumed += 1
                return t

            def done(self):
                self.released = self.consumed
                self.prime()

        def run_pipeline(units, look=2):
            res = {}
            n = len(units)
            for i in range(min(look, n)):
                res[i] = units[i][0]()
            for i in range(n):
                if i + look < n:
                    res[i + look] = units[i + look][0]()
                units[i][1](*res.pop(i))

        def next_pf(pool=(0, 1, 2, 3, 4, 5)):
            i = pool[cnt["pf"] % len(pool)]
            cnt["pf"] += 1
            return pf[i], Bpf[i]

        def evac_eng():
            e = ("scalar", "vector")[cnt["evac"] % 2]
            cnt["evac"] += 1
            return e

        def rstd_from_ss(stt, Bs, np_, n, ncol=1, c_ss=0, c_tmp=4, c_out=8):
            P.op("scalar", lambda h: h.activation(out=stt[0:np_, c_tmp:c_tmp + ncol], in_=stt[0:np_, c_ss:c_ss + ncol],
                                                  func=AF.Sqrt, scale=1.0 / n, bias=epsc[0:np_, :]),
                 reads=[Bs, Beps], writes=[Bs])
            P.op("vector", lambda h: h.reciprocal(out=stt[0:np_, c_out:c_out + ncol], in_=stt[0:np_, c_tmp:c_tmp + ncol]),
                 reads=[Bs], writes=[Bs])

        def norm_tile_T(xt, Bx, np_, mulfn, addfn, zT, col0, Bz):
            stt, Bs = stat()
            yi = cnt["ybf"] % 2
            cnt["ybf"] += 1
            y, By = ybf[yi], Bybf[yi]
            P.op("scalar", lambda h: h.activation(out=junk[0:np_, :], in_=xt, func=AF.Square, accum_out=stt[0:np_, 0:1]),
                 reads=[Bx], writes=[Bjunk, Bs])
            rstd_from_ss(stt, Bs, np_, D)
            P.op("vector", lambda h: h.tensor_scalar(out=y[0:np_, :], in0=xt, scalar1=stt[0:np_, 8:9], scalar2=None, op0=ALU.mult),
                 reads=[Bx, Bs], writes=[By])
            for half in range(2):
                for k8 in range(8):
                    kc = half * 8 + k8
                    P.op("tensor", lambda h, half=half, k8=k8, kc=kc: h.transpose(
                        out=pb[half][:, k8 * 128:k8 * 128 + np_], in_=y[0:np_, kc * 128:(kc + 1) * 128],
                        identity=identb[0:np_, 0:np_]), reads=[By, Bidb], writes=[Bpb[half]])
                for k8 in range(8):
                    kc = half * 8 + k8
                    e = ("scalar", "vector")[half]
                    src = pb[half][:, k8 * 128:k8 * 128 + np_]
                    dst = zT[:, kc, col0:col0 + np_]
                    if e == "scalar":
                        P.op("scalar", lambda h, src=src, dst=dst, kc=kc: h.activation(
                            out=dst, in_=src, func=AF.Identity, scale=mulfn(kc), bias=addfn(kc)),
                            reads=[Bpb[half], Bmods, Bada], writes=[Bz])
                    else:
                        P.op("vector", lambda h, src=src, dst=dst, kc=kc: h.tensor_scalar(
                            out=dst, in0=src, scalar1=mulfn(kc), scalar2=addfn(kc), op0=ALU.mult, op1=ALU.add),
                            reads=[Bpb[half], Bmods, Bada], writes=[Bz])

        def load_x_tile(src_ap, np_=128):
            i = cnt["xst"] % 2
            cnt["xst"] += 1
            dst = xst[i][0:np_, :]
            P.dma("sync", lambda h: h.dma_start(out=dst, in_=src_ap), writes=[Bxst[i]])
            return dst, Bxst[i]

        def load_rope(t0):
            i = cnt["rt"] % 4
            cnt["rt"] += 1
            r, Br = rtabB[i], BrtabB[i]
            P.dma("sync", lambda h: h.dma_start(out=r[:, 0:128], in_=ropeBC_d[t0:t0 + 128, :]), writes=[Br])
            P.dma("sync", lambda h: h.dma_start(out=r[:, 128:256], in_=ropeBS_d[t0:t0 + 128, :]), writes=[Br])
            P.dma("sync", lambda h: h.dma_start(out=r[:, 256:320], in_=ropeAC_d[t0:t0 + 128, 0:64]), writes=[Br])
            P.dma("sync", lambda h: h.dma_start(out=r[:, 320:384], in_=ropeAS_d[t0:t0 + 128, 0:64]), writes=[Br])
            return {"AC": r[:, 256:320], "AS": r[:, 320:384], "BC": r[:, 0:128], "BS": r[:, 128:256], "B": Br}

        def load_ropeA(t0):
            i = cnt["rtA"] % 1
            cnt["rtA"] += 1
            r, Br = rtabA[i], BrtabA[i]
            P.dma("sync", lambda h: h.dma_start(out=r[:, 0:512], in_=ropeAC_d[t0:t0 + 128, :]), writes=[Br])
            P.dma("sync", lambda h: h.dma_start(out=r[:, 512:1024], in_=ropeAS_d[t0:t0 + 128, :]), writes=[Br])
            return {"AC": r[:, 0:512], "AS": r[:, 512:1024], "B": Br}

        def rope(src, Bsrc, W, q, Ct, St, Brt, tmp_t, tmp_u, Btmp, out_ap, Bout):
            g = W // (2 * q)

            def v(ap):
                return ap.rearrange("p (g h i) -> p g h i", g=g, h=2)
            P.op("gpsimd", lambda h: h.tensor_tensor(out=tmp_t, in0=src, in1=Ct, op=ALU.mult),
                 reads=[Bsrc, Brt], writes=[Btmp])
            P.op("vector", lambda h: h.tensor_tensor(out=v(tmp_u)[:, :, 0, :], in0=v(src)[:, :, 1, :], in1=v(St)[:, :, 0, :], op=ALU.mult),
                 reads=[Bsrc, Brt], writes=[Btmp])
            P.op("vector", lambda h: h.tensor_tensor(out=v(tmp_u)[:, :, 1, :], in0=v(src)[:, :, 0, :], in1=v(St)[:, :, 1, :], op=ALU.mult),
                 reads=[Bsrc, Brt], writes=[Btmp])
            P.op("vector", lambda h: h.tensor_tensor(out=out_ap, in0=tmp_t, in1=tmp_u, op=ALU.add),
                 reads=[Btmp], writes=[Bout])

        def transpose_to(src_bf, Bsrc, np_in, nf, dst, Bdst, bank_sel):
            half = bank_sel % 2
            P.op("tensor", lambda h: h.transpose(out=pb[half][0:nf, 0:np_in], in_=src_bf, identity=identb[0:np_in, 0:np_in]),
                 reads=[Bsrc, Bidb], writes=[Bpb[half]])
            e = evac_eng()
            if e == "scalar":
                P.op("scalar", lambda h: h.copy(out=dst, in_=pb[half][0:nf, 0:np_in]), reads=[Bpb[half]], writes=[Bdst])
            else:
                P.op("vector", lambda h: h.tensor_copy(out=dst, in_=pb[half][0:nf, 0:np_in]), reads=[Bpb[half]], writes=[Bdst])

        P.dma("sync", lambda h: h.dma_start(out=identf[:], in_=identf_d), writes=[Bidf])
        P.dma("sync", lambda h: h.dma_start(out=cvec[:], in_=cvec_d), writes=[Bcvec])
        P.dma("sync", lambda h: h.dma_start(out=badaT[:], in_=bada_d), writes=[Bbada])
        P.dma("sync", lambda h: h.dma_start(out=n1g[:], in_=n1g_d), writes=[Bn1g])
        P.dma("sync", lambda h: h.dma_start(out=n2g[:], in_=n2g_d), writes=[Bn2g])
        P.dma("sync", lambda h: h.dma_start(out=convp[:].rearrange("p a b -> p (a b)"), in_=convp_d), writes=[Bconv])
        P.dma("sync", lambda h: h.dma_start(out=gq_b[:], in_=gq_d.partition_broadcast(128)), writes=[Bgq])
        P.dma("sync", lambda h: h.dma_start(out=gkv_b[:], in_=gkv_d.partition_broadcast(128)), writes=[Bgkv])
        P.dma("sync", lambda h: h.dma_start(out=gqq_b[:], in_=gqq_d.partition_broadcast(128)), writes=[Bgqq])
        P.dma("sync", lambda h: h.dma_start(out=gqk_b[:], in_=gqk_d.partition_broadcast(128)), writes=[Bgqk])
        P.op("gpsimd", lambda h: h.tensor_copy(out=identb[:], in_=identf[:]), reads=[Bidf], writes=[Bidb])
        P.op("gpsimd", lambda h: h.memset(onesb[:], 1.0), writes=[Bones])
        P.op("gpsimd", lambda h: h.memset(epsc[:], EPS), writes=[Beps])

        if stop < 0:
            dbg_out("gq", gq_b[:], [128, 768], Bgq)
            dbg_out("conv", convp[:].rearrange("p a b -> p (a b)"), [128, 352], Bconv)
            dbg_out("idb", identb[:], [128, 128], Bidb, BF16)
            P.emit(final_bufs=dbg_bufs)
            return nc, dbg_outs
        P.op("scalar", lambda h: h.activation(out=csb[:], in_=cvec[:], func=AF.Silu), reads=[Bcvec], writes=[Bcsb])
        pada, Bpada = pf[5], Bpf[5]
        ws0 = WStream([[wrows(wada_d, 0, 16, t * 128, 128)] for t in range(96)])
        for t in range(96):
            wt, Bw = ws0.next()
            wv = wt.rearrange("p (k n) -> p k n", k=16)
            for kc in range(16):
                P.op("tensor", lambda h, wv=wv, kc=kc, t=t: h.matmul(
                    pada[:, t * 2:t * 2 + 2], lhsT=wv[:, kc, :], rhs=csb[:, kc * 2:kc * 2 + 2],
                    start=(kc == 0), stop=(kc == 15)), reads=[Bw, Bcsb], writes=[Bpada])
            ws0.done()
        padav = pada[:, 0:192].rearrange("p (j m) -> p j m", m=2)
        for m in range(2):
            P.op("vector", lambda h, m=m: h.tensor_tensor(out=adaT[:, :, m], in0=padav[:, :, m], in1=badaT[:], op=ALU.add),
                 reads=[Bpada, Bbada], writes=[Bada])
        P.op("vector", lambda h: h.scalar_tensor_tensor(out=a1[:], in0=adaT[:, 16:32, 0], scalar=1.0, in1=n1g[:], op0=ALU.add, op1=ALU.mult),
             reads=[Bada, Bn1g], writes=[Bmods])
        P.op("vector", lambda h: h.scalar_tensor_tensor(out=a1c[:], in0=adaT[:, 16:32, 1], scalar=1.0, in1=n1g[:], op0=ALU.add, op1=ALU.mult),
             reads=[Bada, Bn1g], writes=[Bmods])
        P.op("vector", lambda h: h.scalar_tensor_tensor(out=a2[:], in0=adaT[:, 64:80, 0], scalar=1.0, in1=n2g[:], op0=ALU.add, op1=ALU.mult),
             reads=[Bada, Bn2g], writes=[Bmods])
        dbg_out("adaT", adaT[:].rearrange("p a b -> p (a b)"), [128, 192], Bada)

        def ada(term, kc, m=0):
            return adaT[:, term * 16 + kc, m:m + 1]

        if stop < 1:
            P.emit(final_bufs=dbg_bufs)
            return nc, dbg_outs
        P.barrier()
        AR.reset()
        cast_engs[:] = ["vector", "scalar"]
        xst[0] = AR.f32(D); xst[1] = AR.f32(D)
        zT = AR.bf(16 * TB).rearrange("p (k t) -> p k t", k=16); BzT = P.buf("zT")
        ckvT = AR.bf(4 * TB).rearrange("p (k t) -> p k t", k=4); BckvT = P.buf("ckvT")
        wkup = AR.bf(4 * 1024).rearrange("p (k n) -> p k n", k=4); Bwkup = P.buf("wkup")
        wvup = AR.bf(4 * 1024).rearrange("p (k n) -> p k n", k=4); Bwvup = P.buf("wvup")
        kvn = [AR.bf(512) for _ in range(4)]; Bkvn = P.bufs("kvn", 4)
        ktok = [AR.bf(128) for _ in range(2)]; Bktok = P.bufs("ktok", 2)
        raw = [AR.f32(128) for _ in range(2)]; Braw = P.bufs("raw", 2)
        rt_t = [AR.f32(128) for _ in range(2)]
        rt_u = [AR.f32(128) for _ in range(2)]
        rt_o = [AR.f32(128) for _ in range(2)]
        Brt_tmp = P.bufs("rtmp", 2); Brt_o = P.bufs("rto", 2)
        vtok = [AR.bf(1024) for _ in range(2)]; Bvtok = P.bufs("vtok", 2)
        kblk = [AR.bf(TB) for _ in range(2)]; Bkblk = P.bufs("kblk", 2)
        print("phase1 arena words", AR.off)
        wkvv = wkvup_d.rearrange("(k p) (h t d) -> p k h t d", p=128, t=2, d=128)
        for kc in range(4):
            for hh in range(2):
                load_w([wkvv[:, kc, hh * 4:(hh + 1) * 4, 0, :]], dst=wkup[:, kc, hh * 512:(hh + 1) * 512], Bdst=Bwkup)
                load_w([wkvv[:, kc, hh * 4:(hh + 1) * 4, 1, :]], dst=wvup[:, kc, hh * 512:(hh + 1) * 512], Bdst=Bwvup)

        stop_at(0.1)
        kv_tiles = [(0, 128), (128, 128), (256, 128), (384, 128), (512, 64), (576, 128), (704, 128), (832, 128), (960, 128)]
        groups = [(0, 2, True)] + [(2 + 4 * g, 4, False) for g in range(4)]
        unit = 0
        ws1 = WStream([[wrows(win_d, 0, 16, c0, ncol)] for _g in groups for (c0, ncol) in kv_tiles])
        for (kt0, ntl, is_ctx) in groups:
            ntok = ntl * 128
            m = 1 if is_ctx else 0
            ropes = [] if is_ctx else [load_rope((kt0 - 2 + ti_) * 128) for ti_ in range(ntl)]
            for ti in range(ntl):
                if is_ctx:
                    src = ctx_d[ti * 128:(ti + 1) * 128, :]
                else:
                    src = x_d[(kt0 - 2 + ti) * 128:(kt0 - 2 + ti + 1) * 128, :]
                xt, Bx = load_x_tile(src)
                mulfn = (lambda kc: a1c[:, kc:kc + 1]) if is_ctx else (lambda kc: a1[:, kc:kc + 1])
                addfn = (lambda kc, m=m: ada(0, kc, m))
                norm_tile_T(xt, Bx, 128, mulfn, addfn, zT, ti * 128, BzT)
            if dbg and kt0 == 2:
                dbg_out("zT", zT[:, 0, :], [128, TB], BzT, BF16)
            stop_at(0.2)
            sstats = [(sst_t[i_], Bsst[i_]) for i_ in range(ntl)]
            units = []
            cur = {}
            for ci, (c0, ncol) in enumerate(kv_tiles):
                for ti in range(ntl):
                    def mm(ci=ci, ncol=ncol, ti=ti, ntl=ntl):
                        if ti == 0:
                            wt, Bw = ws1.next()
                            cur["w"] = (wt.rearrange("p (k n) -> p k n", k=16), Bw)
                        wv, Bw = cur["w"]
                        pt, Bp = next_pf((0, 1, 2, 3))
                        for kc in range(16):
                            P.op("tensor", lambda h, pt=pt, wv=wv, kc=kc, ti=ti, ncol=ncol: h.matmul(
                                pt[:, 0:ncol], lhsT=zT[:, kc, ti * 128:(ti + 1) * 128], rhs=wv[:, kc, :],
                                start=(kc == 0), stop=(kc == 15)), reads=[BzT, Bw], writes=[Bp])
                        if ti == ntl - 1:
                            ws1.done()
                        return pt, Bp

                    def post(pt, Bp, ci=ci, ti=ti, kt0=kt0, ntl=ntl, is_ctx=is_ctx, ropes=ropes, sstats=sstats, ntok=ntok):
                        kt = kt0 + ti
                        stt, Bs = sstats[ti]
                        u2 = cnt["unit"] % 2
                        cnt["unit"] += 1
                        unit = cnt["unit"]
                        if ci < 4:
                            P.op("scalar", lambda h, pt=pt, stt=stt, ci=ci: h.activation(
                                out=junk[:, 0:128], in_=pt[:, 0:128], func=AF.Square, accum_out=stt[:, ci:ci + 1]),
                                reads=[Bp], writes=[Bjunk, Bs])
                            P.op("vector", lambda h, pt=pt, ti=ti, ci=ci: h.tensor_tensor(
                                out=kvn[ti][:, ci * 128:(ci + 1) * 128], in0=pt[:, 0:128], in1=gkv_b[:, ci * 128:(ci + 1) * 128], op=ALU.mult),
                                reads=[Bp, Bgkv], writes=[Bkvn[ti]])
                            if ci == 3:
                                P.op("scalar", lambda h, stt=stt: h.activation(out=stt[:, 12:16], in_=stt[:, 0:4], func=AF.Identity, accum_out=stt[:, 5:6]),
                                     reads=[Bs], writes=[Bs])
                                rstd_from_ss(stt, Bs, 128, 512, c_ss=5, c_tmp=6, c_out=8)
                                P.op("vector", lambda h, ti=ti, stt=stt: h.tensor_scalar(
                                    out=kvn[ti][:], in0=kvn[ti][:], scalar1=stt[:, 8:9], scalar2=None, op0=ALU.mult),
                                    reads=[Bs, Bkvn[ti]], writes=[Bkvn[ti]])
                                for c4 in range(4):
                                    transpose_to(kvn[ti][:, c4 * 128:(c4 + 1) * 128], Bkvn[ti], 128, 128,
                                                 ckvT[:, c4, ti * 128:(ti + 1) * 128], BckvT, c4)
                        elif ci == 4:
                            if is_ctx:
                                P.op("scalar", lambda h, pt=pt, u2=u2: h.copy(out=ktok[u2][:, 0:64], in_=pt[:, 0:64]), reads=[Bp], writes=[Bktok[u2]])
                            else:
                                if ti >= len(ropes):
                                    ropes.append(load_rope((kt - 2) * 128))
                                rp = ropes[ti]
                                P.op("scalar", lambda h, pt=pt, u2=u2: h.activation(out=raw[u2][:, 0:64], in_=pt[:, 0:64], func=AF.Identity), reads=[Bp], writes=[Braw[u2]])
                                rope(raw[u2][:, 0:64], Braw[u2], 64, 16, rp["AC"], rp["AS"], rp["B"],
                                     rt_t[u2][:, 0:64], rt_u[u2][:, 0:64], Brt_tmp[u2], rt_o[u2][:, 0:64], Brt_o[u2])
                                P.op("scalar", lambda h, u2=u2: h.copy(out=ktok[u2][:, 0:64], in_=rt_o[u2][:, 0:64]), reads=[Brt_o[u2]], writes=[Bktok[u2]])
                                if dbg and kt == 2:
                                    dbg_out("kp_raw", raw[u2][:, 0:64], [128, 64], Braw[u2])
                                    dbg_out("kp_t", rt_t[u2][:, 0:64], [128, 64], Brt_tmp[u2])
                                    dbg_out("kp_u", rt_u[u2][:, 0:64], [128, 64], Brt_tmp[u2])
                                    dbg_out("kp_o", rt_o[u2][:, 0:64], [128, 64], Brt_o[u2])
                                    dbg_out("kp_tab", rp["AC"], [128, 64], rp["B"])
                            P.op("gpsimd", lambda h, u2=u2: h.tensor_copy(out=ktok[u2][:, 64:128], in_=ktok[u2][:, 0:64]), reads=[Bktok[u2]], writes=[Bktok[u2]])
                            transpose_to(ktok[u2][:, 0:128], Bktok[u2], 128, 128, kpeT[:, kt * 128:(kt + 1) * 128], Bkpe, unit)
                        elif ci in (5, 6):
                            hb = ci - 5
                            st2, Bs2 = stat()
                            P.op("scalar", lambda h, pt=pt, st2=st2: h.activation(
                                out=junk[:, 0:128], in_=pt[:, 0:128], func=AF.Square, accum_out=st2[:, 0:1]),
                                reads=[Bp], writes=[Bjunk, Bs2])
                            rstd_from_ss(st2, Bs2, 128, 128)
                            P.op("vector", lambda h, pt=pt, u2=u2: h.tensor_tensor(out=raw[u2][:], in0=pt[:, 0:128], in1=gqk_b[:], op=ALU.mult),
                                 reads=[Bp, Bgqk], writes=[Braw[u2]])
                            if is_ctx:
                                P.op("scalar", lambda h, u2=u2, st2=st2: h.activation(out=ktok[u2][:], in_=raw[u2][:], func=AF.Identity, scale=st2[:, 8:9]),
                                     reads=[Braw[u2], Bs2], writes=[Bktok[u2]])
                            else:
                                rp = ropes[ti]
                                rope(raw[u2][:], Braw[u2], 128, 32, rp["BC"], rp["BS"], rp["B"],
                                     rt_t[u2][:], rt_u[u2][:], Brt_tmp[u2], rt_o[u2][:], Brt_o[u2])
                                P.op("scalar", lambda h, u2=u2, st2=st2: h.activation(out=ktok[u2][:], in_=rt_o[u2][:], func=AF.Identity, scale=st2[:, 8:9]),
                                     reads=[Brt_o[u2], Bs2], writes=[Bktok[u2]])
                            kb = kblk[hb]
                            transpose_to(ktok[u2][:], Bktok[u2], 128, 128, kb[:, ti * 128:(ti + 1) * 128], Bkblk[hb], unit)
                            if ti == ntl - 1:
                                P.dma("scalar", lambda h, hb=hb, kb=kb, kt0=kt0, ntok=ntok: h.dma_start(
                                    out=kas_d[8 + hb, :, kt0 * 128:kt0 * 128 + ntok], in_=kb[:, 0:ntok]), reads=[Bkblk[hb]], sembuf=Bkblk[hb])
                        else:
                            hb = ci - 7
                            vt = vtok[u2]
                            P.op("scalar", lambda h, pt=pt, vt=vt: h.copy(out=vt[:, 0:128], in_=pt[:, 0:128]), reads=[Bp], writes=[Bvtok[u2]])
                            P.dma("scalar", lambda h, hb=hb, vt=vt, kt=kt: h.dma_start(
                                out=vas_d[8 + hb, :, kt * 128:(kt + 1) * 128], in_=vt[:, 0:128]), reads=[Bvtok[u2]], sembuf=Bvtok[u2])
                    units.append((mm, post))
            run_pipeline(units)
            stop_at(0.4)
            for hh in range(8):
                pt, Bp = next_pf((0, 1, 2, 3))
                for kc in range(4):
                    P.op("tensor", lambda h, pt=pt, kc=kc, hh=hh, ntok=ntok: h.matmul(
                        pt[:, 0:ntok], lhsT=wkup[:, kc, hh * 128:(hh + 1) * 128], rhs=ckvT[:, kc, 0:ntok],
                        start=(kc == 0), stop=(kc == 3)), reads=[Bwkup, BckvT], writes=[Bp])
                kb = kblk[hh % 2]
                Bkb = Bkblk[hh % 2]
                e = evac_eng()
                if e == "scalar":
                    P.op("scalar", lambda h, pt=pt, kb=kb, ntok=ntok: h.copy(out=kb[:, 0:ntok], in_=pt[:, 0:ntok]), reads=[Bp], writes=[Bkb])
                else:
                    P.op("vector", lambda h, pt=pt, kb=kb, ntok=ntok: h.tensor_copy(out=kb[:, 0:ntok], in_=pt[:, 0:ntok]), reads=[Bp], writes=[Bkb])
                P.dma("scalar", lambda h, hh=hh, kb=kb, kt0=kt0, ntok=ntok: h.dma_start(
                    out=kas_d[hh, :, kt0 * 128:kt0 * 128 + ntok], in_=kb[:, 0:ntok]), reads=[Bkb], sembuf=Bkb)
            for ti in range(ntl):
                kt = kt0 + ti
                vt = vtok[ti % 2]
                Bvt = Bvtok[ti % 2]
                for half in range(2):
                    pt, Bp = next_pf((0, 1, 2, 3))
                    for kc in range(4):
                        P.op("tensor", lambda h, pt=pt, kc=kc, ti=ti, half=half: h.matmul(
                            pt[:, 0:512], lhsT=ckvT[:, kc, ti * 128:(ti + 1) * 128], rhs=wvup[:, kc, half * 512:(half + 1) * 512],
                            start=(kc == 0), stop=(kc == 3)), reads=[Bwvup, BckvT], writes=[Bp])
                    e = evac_eng()
                    if e == "scalar":
                        P.op("scalar", lambda h, pt=pt, vt=vt, half=half: h.copy(out=vt[:, half * 512:(half + 1) * 512], in_=pt[:, 0:512]), reads=[Bp], writes=[Bvt])
                    else:
                        P.op("vector", lambda h, pt=pt, vt=vt, half=half: h.tensor_copy(out=vt[:, half * 512:(half + 1) * 512], in_=pt[:, 0:512]), reads=[Bp], writes=[Bvt])
                P.dma("scalar", lambda h, vt=vt, kt=kt: h.dma_start(
                    out=vas_d[0:8, :, kt * 128:(kt + 1) * 128].rearrange("h p d -> p h d"),
                    in_=vt[:, 0:1024].rearrange("p (h d) -> p h d", h=8)), reads=[Bvt], sembuf=Bvt)
        if dbg:
            dbg_out("kpeT", kpeT[0:64, :], [64, TK], Bkpe, BF16)

        if stop < 2:
            P.emit(final_bufs=dbg_bufs)
            return nc, dbg_outs
        P.barrier()
        AR.reset()
        if dbg:
            for nm, ap_ in (("kas0", kas_d[0]), ("kas8", kas_d[8]), ("vas0", vas_d[0]), ("vas8", vas_d[8])):
                dbg_out(nm, ap_, [128, TK], P.buf("dummy_" + nm), BF16)
        cast_engs[:] = ["vector", "scalar"]
        xst[0] = AR.f32(D); xst[1] = xst[0]
        Bxst[1] = Bxst[0]
        zT2 = AR.bf(16 * TB).rearrange("p (k t) -> p k t", k=16); BzT2 = P.buf("zT2")
        qoff = AR.off
        qaT = AR.bf(8 * TB).rearrange("p (h t) -> p h t", h=8); BqaT = P.buf("qaT")
        qpT = AR.bf(4 * TB).rearrange("p (h t) -> p h t", h=4); BqpT = P.buf("qpT")
        qbT = AR.bf(8 * TB).rearrange("p (h t) -> p h t", h=8); BqbT = P.buf("qbT")
        mT = arena[:, qoff:qoff + 8 * TB].bitcast(BF16).rearrange("p (k t) -> p k t", k=16); BmT = P.buf("mT")
        oT = AR.bf(16 * TB).rearrange("p (h t) -> p h t", h=16); BoT = P.buf("oT")
        kst = [AR.bf(TK) for _ in range(2)]; Bkst = P.bufs("kst", 2)
        vst = [AR.bf(NKT * 128).rearrange("p (k d) -> p k d", k=NKT) for _ in range(2)]; Bvst = P.bufs("vst", 2)
        pT = [AR.bf(TB) for _ in range(4)]; BpT = P.bufs("pT", 4)
        cqg = AR.bf(4 * 768).rearrange("p (t n) -> p t n", t=4); Bcqg = P.bufs("cqg", 4)
        cqT = AR.bf(6 * TB).rearrange("p (k t) -> p k t", k=6); BcqT = P.buf("cqT")
        qatok = [AR.bf(1024) for _ in range(2)]; Bqatok = P.bufs("qatok", 2)
        qpe = [AR.f32(512) for _ in range(1)] * 2; Bqpe = P.bufs("qpe", 1) * 2
        qpe_t = [AR.f32(512) for _ in range(1)] * 2
        qpe_u = [AR.f32(512) for _ in range(1)] * 2
        qpe_o = [AR.f32(512) for _ in range(1)] * 2
        Bqpe_tmp = P.bufs("qpetmp", 1) * 2; Bqpe_o = P.bufs("qpeo", 1) * 2
        qpb = [AR.bf(512) for _ in range(1)] * 2; Bqpb = P.bufs("qpb", 1) * 2
        rawq = [AR.f32(128) for _ in range(2)]; Brawq = P.bufs("raw2", 2)
        rq_t = [AR.f32(128) for _ in range(2)]
        rq_u = [AR.f32(128) for _ in range(2)]
        rq_o = [AR.f32(128) for _ in range(2)]
        Brq_tmp = P.bufs("rtmp2", 2); Brq_o = P.bufs("rto2", 2)
        qtok = [AR.bf(128) for _ in range(2)]; Bqtok = P.bufs("qtok", 2)
        toff = AR.off
        tail = [AR.f32(TB) for _ in range(4)]
        rden = tail[0:2]; Brden = P.bufs("rden", 2)
        gsig = tail[0:2]; Bgsig = P.bufs("gsig", 2)
        mtmp = tail[2:4]; Bmtmp = P.bufs("mtmp", 2)
        aoT2 = tail[0:2]; BaoT2 = P.bufs("aoT2", 2)
        xres = [tail[2].rearrange("p (t d) -> p t d", t=4), tail[3].rearrange("p (t d) -> p t d", t=4)]; Bxres = P.bufs("xres", 2)
        print("phase2 arena words", AR.off)
        wqup_v = wqup_d.rearrange("(k p) n -> p k n", p=128)

        q_tiles = [(KV_COLS + i * 128, 128) for i in range(14)]
        GA0 = KV_COLS + Q_COLS
        specs2 = []
        for _qb in range(NB):
            for (c0, ncol) in q_tiles:
                specs2.append([wrows(win_d, 0, 16, c0, ncol)])
            for _ti in range(4):
                for c3 in range(4):
                    for kh in range(2):
                        specs2.append([wqup_v[:, kh * 3:(kh + 1) * 3, c3 * 384:(c3 + 1) * 384]])
            for j in range(16):
                specs2.append([wrows(wbra_d, 0, 8, j * 128, 128), wrows(wbrb_d, 0, 8, j * 128, 128)])
                specs2.append([wrows(win_d, 0, 16, GA0 + j * 128, 128)])
                specs2.append([wrows(win_d, 0, 16, GA0 + D + j * 128, 128)])
            for jo in range(16):
                specs2.append([wrows(wout_d, 0, 16, jo * 128, 128)])
        ws2 = WStream(specs2)
        SCA = 192.0 ** -0.5
        SCB = 128.0 ** -0.5
        for qb in range(NB):
            t0 = qb * TB
            ropes = []
            if qb > 0:
                P.barrier()
            for ti in range(4):
                xt, Bx = load_x_tile(x_d[t0 + ti * 128:t0 + (ti + 1) * 128, :])
                norm_tile_T(xt, Bx, 128, lambda kc: a1[:, kc:kc + 1], lambda kc: ada(0, kc, 0), zT2, ti * 128, BzT2)
                ropes.append(load_rope(t0 + ti * 128))
            sstats = [(sst_t[i_], Bsst[i_]) for i_ in range(4)]
            unit = 0
            units = []
            cur = {}
            for ci, (c0, ncol) in enumerate(q_tiles):
                for ti in range(4):
                    def mm(ci=ci, ti=ti):
                        if ti == 0:
                            wt, Bw = ws2.next()
                            cur["w"] = (wt.rearrange("p (k n) -> p k n", k=16), Bw)
                        wv, Bw = cur["w"]
                        pt, Bp = next_pf((0, 1, 2, 3))
                        for kc in range(16):
                            P.op("tensor", lambda h, pt=pt, wv=wv, kc=kc, ti=ti: h.matmul(
                                pt[:, 0:128], lhsT=zT2[:, kc, ti * 128:(ti + 1) * 128], rhs=wv[:, kc, :],
                                start=(kc == 0), stop=(kc == 15)), reads=[BzT2, Bw], writes=[Bp])
                        if ti == 3:
                            ws2.done()
                        return pt, Bp

                    def post(pt, Bp, ci=ci, ti=ti, ropes=ropes, sstats=sstats):
                        u2 = cnt["unit"] % 2
                        cnt["unit"] += 1
                        unit = cnt["unit"]
                        if ci < 6:
                            stt, Bs = sstats[ti]
                            P.op("scalar", lambda h, pt=pt, stt=stt, ci=ci: h.activation(
                                out=junk[:, 0:128], in_=pt[:, 0:128], func=AF.Square, accum_out=stt[:, ci:ci + 1]),
                                reads=[Bp], writes=[Bjunk, Bs])
                            P.op("vector", lambda h, pt=pt, ti=ti, ci=ci: h.tensor_tensor(
                                out=cqg[:, ti, ci * 128:(ci + 1) * 128], in0=pt[:, 0:128], in1=gq_b[:, ci * 128:(ci + 1) * 128], op=ALU.mult),
                                reads=[Bp, Bgq], writes=[Bcqg[ti]])
                            transpose_to(cqg[:, ti, ci * 128:(ci + 1) * 128], Bcqg[ti], 128, 128, cqT[:, ci, ti * 128:(ti + 1) * 128], BcqT, unit)
                            if ci == 5:
                                P.op("scalar", lambda h, stt=stt: h.activation(out=stt[:, 9:15], in_=stt[:, 0:6], func=AF.Identity, accum_out=stt[:, 6:7]),
                                     reads=[Bs], writes=[Bs])
                                rstd_from_ss(stt, Bs, 128, 768, c_ss=6, c_tmp=7, c_out=8)
                        else:
                            hq = ci - 6
                            st2, Bs2 = stat()
                            P.op("scalar", lambda h, pt=pt, st2=st2: h.activation(
                                out=junk[:, 0:128], in_=pt[:, 0:128], func=AF.Square, accum_out=st2[:, 0:1]),
                                reads=[Bp], writes=[Bjunk, Bs2])
                            rstd_from_ss(st2, Bs2, 128, 128)
                            P.op("vector", lambda h, pt=pt, u2=u2: h.tensor_tensor(out=rawq[u2][:], in0=pt[:, 0:128], in1=gqq_b[:], op=ALU.mult),
                                 reads=[Bp, Bgqq], writes=[Brawq[u2]])
                            rp = ropes[ti]
                            rope(rawq[u2][:], Brawq[u2], 128, 32, rp["BC"], rp["BS"], rp["B"],
                                 rq_t[u2][:], rq_u[u2][:], Brq_tmp[u2], rq_o[u2][:], Brq_o[u2])
                            P.op("scalar", lambda h, u2=u2, st2=st2: h.activation(out=qtok[u2][:], in_=rq_o[u2][:], func=AF.Identity, scale=st2[:, 8:9]),
                                 reads=[Brq_o[u2], Bs2], writes=[Bqtok[u2]])
                            transpose_to(qtok[u2][:], Bqtok[u2], 128, 128, qbT[:, hq, ti * 128:(ti + 1) * 128], BqbT, unit)
                    units.append((mm, post))
            run_pipeline(units)
            units = []
            for ti in range(4):
                for c3 in range(4):
                    def mm(ti=ti, c3=c3):
                        wq3 = [ws2.next(), ws2.next()]
                        pt, Bp = next_pf((0, 1, 2, 3))
                        for kc in range(6):
                            wt, Bw = wq3[kc // 3]
                            wv = wt.rearrange("p (k n) -> p k n", k=3)
                            P.op("tensor", lambda h, pt=pt, wv=wv, kc=kc, ti=ti: h.matmul(
                                pt[:, 0:384], lhsT=cqT[:, kc, ti * 128:(ti + 1) * 128], rhs=wv[:, kc % 3, :],
                                start=(kc == 0), stop=(kc == 5)), reads=[BcqT, Bw], writes=[Bp])
                        ws2.done()
                        return pt, Bp

                    def post(pt, Bp, ti=ti, c3=c3, sstats=sstats, t0=t0):
                        stt, Bs = sstats[ti]
                        qi = ti % 2
                        ptv = pt[:, 0:384].rearrange("p (h d) -> p h d", h=2)
                        P.op("scalar", lambda h, ptv=ptv, qi=qi, c3=c3, stt=stt: h.activation(
                            out=qatok[qi][:, c3 * 256:(c3 + 1) * 256].rearrange("p (h d) -> p h d", h=2), in_=ptv[:, :, 0:128],
                            func=AF.Identity, scale=stt[:, 8:9]), reads=[Bp, Bs], writes=[Bqatok[qi]])
                        P.op("scalar", lambda h, ptv=ptv, qi=qi, c3=c3, stt=stt: h.activation(
                            out=qpe[qi][:, c3 * 128:(c3 + 1) * 128].rearrange("p (h d) -> p h d", h=2), in_=ptv[:, :, 128:192],
                            func=AF.Identity, scale=stt[:, 8:9]), reads=[Bp, Bs], writes=[Bqpe[qi]])
                        if c3 == 3:
                            rp = load_ropeA(t0 + ti * 128)
                            rope(qpe[qi][:], Bqpe[qi], 512, 16, rp["AC"], rp["AS"], rp["B"],
                                 qpe_t[qi][:], qpe_u[qi][:], Bqpe_tmp[qi], qpe_o[qi][:], Bqpe_o[qi])
                            P.op("gpsimd", lambda h, qi=qi: h.tensor_copy(out=qpb[qi][:], in_=qpe_o[qi][:]), reads=[Bqpe_o[qi]], writes=[Bqpb[qi]])
                            for hh in range(8):
                                transpose_to(qatok[qi][:, hh * 128:(hh + 1) * 128], Bqatok[qi], 128, 128, qaT[:, hh, ti * 128:(ti + 1) * 128], BqaT, hh)
                            for g4 in range(4):
                                half = g4 % 2
                                P.op("tensor", lambda h, qi=qi, g4=g4, half=half: h.transpose(
                                    out=pb[half][0:64, 0:128], in_=qpb[qi][:, g4 * 64:(g4 + 1) * 64], identity=identb[:]),
                                    reads=[Bqpb[qi], Bidb], writes=[Bpb[half]])
                                P.op("tensor", lambda h, qi=qi, g4=g4, half=half: h.transpose(
                                    out=pb[half][64:128, 0:128], in_=qpb[qi][:, (g4 + 4) * 64:(g4 + 5) * 64], identity=identb[:]),
                                    reads=[Bqpb[qi], Bidb], writes=[Bpb[half]])
                                P.op("vector", lambda h, g4=g4, half=half, ti=ti: h.tensor_copy(
                                    out=qpT[:, g4, ti * 128:(ti + 1) * 128], in_=pb[half][:, 0:128]), reads=[Bpb[half]], writes=[BqpT])
                    units.append((mm, post))
            run_pipeline(units)
            if dbg and qb == 0:
                dbg_out("qaT", qaT[:, 0, :], [128, TB], BqaT, BF16)
                dbg_out("qpT", qpT[:, 0, :], [128, TB], BqpT, BF16)
                dbg_out("qbT", qbT[:, 0, :], [128, TB], BqbT, BF16)
            LOOK = 3
            pb32 = [pb[0][:, :].bitcast(F32), pb[1][:, :].bitcast(F32)]
            acc_sets = [((pf[4][:, :], Bpf[4]), (pf[5][:, :], Bpf[5])), ((pb32[0], Bpb[0]), (pb32[1], Bpb[1]))]
            steps = [(hidx, kt) for hidx in range(16) for kt in range(NKT)]
            hinfo = {}
            hcount = 0
            for hidx in range(16):
                isA = hidx < 8
                hh = hidx if isA else hidx - 8
                sh = hh if isA else 8 + hh // 4
                need_load = isA or (hh % 4 == 0)
                if need_load:
                    ki = hcount % 2
                    hcount += 1
                hinfo[hidx] = (isA, hh, sh, need_load, ki)
            Sq = {}

            def emit_S(n):
                hidx, kt = steps[n]
                isA, hh, sh, need_load, ki = hinfo[hidx]
                if kt == 0 and need_load:
                    P.dma("sync", lambda h, ki=ki, sh=sh: h.dma_start(out=kst[ki][:], in_=kas_d[sh]), writes=[Bkst[ki]])
                    P.dma("sync", lambda h, ki=ki, sh=sh: h.dma_start(out=vst[ki][:].rearrange("p k d -> p (k d)"), in_=vas_d[sh]), writes=[Bvst[ki]])
                pS, BpS = next_pf((0, 1, 2, 3))
                if isA:
                    P.op("tensor", lambda h, pS=pS, ki=ki, kt=kt, hh=hh: h.matmul(
                        pS[:, :], lhsT=kst[ki][:, kt * 128:(kt + 1) * 128], rhs=qaT[:, hh, :], start=True, stop=False),
                        reads=[Bkst[ki], BqaT], writes=[BpS])
                    pl = 0 if hh < 4 else 64
                    P.op("tensor", lambda h, pS=pS, kt=kt, hh=hh, pl=pl: h.matmul(
                        pS[:, :], lhsT=kpeT[pl:pl + 64, kt * 128:(kt + 1) * 128], rhs=qpT[pl:pl + 64, hh % 4, :], start=False, stop=True),
                        reads=[Bkpe, BqpT], writes=[BpS])
                else:
                    P.op("tensor", lambda h, pS=pS, ki=ki, kt=kt, hh=hh: h.matmul(
                        pS[:, :], lhsT=kst[ki][:, kt * 128:(kt + 1) * 128], rhs=qbT[:, hh, :], start=True, stop=True),
                        reads=[Bkst[ki], BqbT], writes=[BpS])
                Sq[n] = (pS, BpS)

            for n in range(min(LOOK, len(steps))):
                emit_S(n)
            for n in range(len(steps)):
                if n + LOOK < len(steps):
                    emit_S(n + LOOK)
                hidx, kt = steps[n]
                isA, hh, sh, need_load, ki = hinfo[hidx]
                (po, Bpo), (pd, Bpd) = acc_sets[hidx % 2]
                pS, BpS = Sq.pop(n)
                pi = n % 4
                P.op("scalar", lambda h, pS=pS, pi=pi, isA=isA: h.activation(
                    out=pT[pi][:], in_=pS[:, :], func=AF.Exp, scale=(SCA if isA else SCB)), reads=[BpS], writes=[BpT[pi]])
                P.op("tensor", lambda h, po=po, ki=ki, kt=kt, pi=pi: h.matmul(
                    po, lhsT=vst[ki][:, kt, :], rhs=pT[pi][:], start=(kt == 0), stop=(kt == NKT - 1)),
                    reads=[Bvst[ki], BpT[pi]], writes=[Bpo])
                P.op("tensor", lambda h, pd=pd, kt=kt, pi=pi: h.matmul(
                    pd, lhsT=onesb[:], rhs=pT[pi][:], start=(kt == 0), stop=(kt == NKT - 1)),
                    reads=[Bones, BpT[pi]], writes=[Bpd])
                if kt == NKT - 1:
                    ri = hidx % 2
                    P.op("scalar", lambda h, pd=pd, ri=ri: h.activation(out=rden[ri][:], in_=pd, func=AF.Ln), reads=[Bpd], writes=[Brden[ri]])
                    P.op("scalar", lambda h, ri=ri: h.activation(out=rden[ri][:], in_=rden[ri][:], func=AF.Exp, scale=-1.0), reads=[Brden[ri]], writes=[Brden[ri]])
                    P.op("vector", lambda h, po=po, ri=ri, hidx=hidx: h.tensor_tensor(out=oT[:, hidx, :], in0=po, in1=rden[ri][:], op=ALU.mult),
                         reads=[Bpo, Brden[ri]], writes=[BoT])
            if dbg and qb == 0:
                dbg_out("oTa", oT[:, 0, :], [128, TB], BoT, BF16)
                dbg_out("oTb", oT[:, 8, :], [128, TB], BoT, BF16)
            P.barrier()
            for j in range(16):
                wa, Bwa = ws2.next()
                wav = wa.rearrange("p (k n) -> p k n", k=16)
                wga, Bwga = ws2.next()
                wgav = wga.rearrange("p (k n) -> p k n", k=16)
                wgb, Bwgb = ws2.next()
                wgbv = wgb.rearrange("p (k n) -> p k n", k=16)
                pa, Bpa = next_pf((0, 1, 2, 3, 4, 5))
                for k in range(8):
                    P.op("tensor", lambda h, pa=pa, wav=wav, k=k: h.matmul(pa[:, :], lhsT=wav[:, k, :], rhs=oT[:, k, :], start=(k == 0), stop=(k == 7)),
                         reads=[Bwa, BoT], writes=[Bpa])
                pbr, Bpbr = next_pf((0, 1, 2, 3, 4, 5))
                for k in range(8):
                    P.op("tensor", lambda h, pbr=pbr, wav=wav, k=k: h.matmul(pbr[:, :], lhsT=wav[:, 8 + k, :], rhs=oT[:, 8 + k, :], start=(k == 0), stop=(k == 7)),
                         reads=[Bwa, BoT], writes=[Bpbr])
                pga, Bpga = next_pf((0, 1, 2, 3, 4, 5))
                for kc in range(16):
                    P.op("tensor", lambda h, pga=pga, wgav=wgav, kc=kc: h.matmul(pga[:, :], lhsT=wgav[:, kc, :], rhs=zT2[:, kc, :], start=(kc == 0), stop=(kc == 15)),
                         reads=[Bwga, BzT2], writes=[Bpga])
                pgb, Bpgb = next_pf((0, 1, 2, 3, 4, 5))
                for kc in range(16):
                    P.op("tensor", lambda h, pgb=pgb, wgbv=wgbv, kc=kc: h.matmul(pgb[:, :], lhsT=wgbv[:, kc, :], rhs=zT2[:, kc, :], start=(kc == 0), stop=(kc == 15)),
                         reads=[Bwgb, BzT2], writes=[Bpgb])
                ws2.done()
                P.op("scalar", lambda h, pga=pga: h.activation(out=gsig[0][:], in_=pga[:, :], func=AF.Sigmoid), reads=[Bpga], writes=[Bgsig[0]])
                P.op("scalar", lambda h, pgb=pgb: h.activation(out=gsig[1][:], in_=pgb[:, :], func=AF.Sigmoid), reads=[Bpgb], writes=[Bgsig[1]])
                P.op("vector", lambda h, pa=pa: h.tensor_tensor(out=mtmp[0][:], in0=pa[:, :], in1=gsig[0][:], op=ALU.mult), reads=[Bpa, Bgsig[0]], writes=[Bmtmp[0]])
                P.op("vector", lambda h, pbr=pbr: h.tensor_tensor(out=mtmp[1][:], in0=pbr[:, :], in1=gsig[1][:], op=ALU.mult), reads=[Bpbr, Bgsig[1]], writes=[Bmtmp[1]])
                P.op("gpsimd", lambda h, j=j: h.tensor_tensor(out=mT[:, j, :], in0=mtmp[0][:], in1=mtmp[1][:], op=ALU.add), reads=[Bmtmp[0], Bmtmp[1]], writes=[BmT])
            if dbg and qb == 0:
                dbg_out("mT", mT[:, 0, :], [128, TB], BmT, BF16)
            P.barrier()
            units = []
            for jo in range(16):
                def mm(jo=jo):
                    wo, Bwo = ws2.next()
                    wov = wo.rearrange("p (k n) -> p k n", k=16)
                    pt, Bp = next_pf((0, 1, 2, 3))
                    for kc in range(16):
                        P.op("tensor", lambda h, pt=pt, wov=wov, kc=kc: h.matmul(pt[:, :], lhsT=wov[:, kc, :], rhs=mT[:, kc, :], start=(kc == 0), stop=(kc == 15)),
                             reads=[Bwo, BmT], writes=[Bp])
                    ws2.done()
                    return pt, Bp

                def post(pt, Bp, jo=jo, t0=t0):
                    ai = jo % 2
                    P.op("scalar", lambda h, pt=pt, ai=ai, jo=jo: h.activation(out=aoT2[ai][:], in_=pt[:, :], func=AF.Identity, scale=ada(2, jo, 0)),
                         reads=[Bp, Bada], writes=[BaoT2[ai]])
                    P.dma("sync", lambda h, ai=ai, jo=jo, t0=t0: h.dma_start(
                        out=xres[ai][:], in_=x_d[t0:t0 + TB, jo * 128:(jo + 1) * 128].rearrange("(t p) d -> p t d", p=128)), writes=[Bxres[ai]])
                    p2, Bp2 = pf[4 + ai], Bpf[4 + ai]
                    for ti in range(4):
                        P.op("tensor", lambda h, p2=p2, ai=ai, ti=ti: h.transpose(out=p2[:, ti * 128:(ti + 1) * 128], in_=aoT2[ai][:, ti * 128:(ti + 1) * 128], identity=identf[:]),
                             reads=[BaoT2[ai], Bidf], writes=[Bp2])
                    P.op("vector", lambda h, p2=p2, ai=ai: h.tensor_tensor(out=xres[ai][:], in0=p2[:, :].rearrange("p (t d) -> p t d", t=4), in1=xres[ai][:], op=ALU.add),
                         reads=[Bp2, Bxres[ai]], writes=[Bxres[ai]])
                    P.dma("scalar", lambda h, ai=ai, jo=jo, t0=t0: h.dma_start(
                        out=x1s_d[t0:t0 + TB, jo * 128:(jo + 1) * 128].rearrange("(t p) d -> p t d", p=128), in_=xres[ai][:]), reads=[Bxres[ai]], sembuf=Bxres[ai])
                units.append((mm, post))
            run_pipeline(units)

        if stop < 3:
            P.emit(final_bufs=dbg_bufs)
            return nc, dbg_outs
        P.barrier()
        AR.reset()
        cast_engs[:] = ["scalar", "vector"]
        z2T = AR.bf(16 * TB).rearrange("p (k t) -> p k t", k=16); Bz2T = P.buf("z2T")
        zh = AR.bf(16 * 8).rearrange("p (k t) -> p k t", k=16); Bzh = P.buf("zh")
        hT = AR.bf(NFC * TB).rearrange("p (f t) -> p f t", f=NFC); BhT = P.buf("hT")
        x2 = AR.f32(4 * D).rearrange("p (t d) -> p t d", t=4); Bx2 = P.bufs("x2", 4)
        uh = AR.f32(88 * 8).rearrange("p (j c) -> p j c", j=88); Buh = P.buf("uh")
        gfin = AR.f32(D); Bgfin = P.buf("gfin")
        ca = [AR.f32(TB) for _ in range(2)]; Bca = P.bufs("ca", 2)
        cb = [AR.f32(TB) for _ in range(2)]; Bcb = P.bufs("cb", 2)
        sa = [AR.f32(TB) for _ in range(2)]; Bsa = P.bufs("sa", 2)
        aoT = [AR.f32(TB) for _ in range(2)]; BaoT = P.bufs("aoT3", 2)
        print("phase3 arena words", AR.off)
        P.dma("sync", lambda h: h.dma_start(out=gfin, in_=gfin_d.partition_broadcast(128)), writes=[Bgfin])
        for bi in range(3):
            P.dma("sync", lambda h, bi=bi: h.dma_start(out=x2[2 * bi:2 * bi + 2, 3, :], in_=x1s_d[512 * (bi + 1) - 1:512 * (bi + 1) + 1, :]), writes=[Bx2[3]])
        norm_tile_T(x2[0:6, 3, :], Bx2[3], 6, lambda kc: a2[:, kc:kc + 1], lambda kc: ada(3, kc, 0), zh, 0, Bzh)
        P.op("gpsimd", lambda h: h.memset(uh[:].rearrange("p j c -> p (j c)"), 0.0), writes=[Buh])
        specs3 = []
        for _blk in range(NB):
            for j in range(NFC):
                specs3.append([wrows(wup_d, 0, 8, j * 128, 128), wrows(wup_d, 0, 8, DFF + j * 128, 128)])
                specs3.append([wrows(wup_d, 8, 8, j * 128, 128), wrows(wup_d, 8, 8, DFF + j * 128, 128)])
            for jo in range(16):
                for (f0, nf) in ((0, 16), (16, 16), (32, 12)):
                    specs3.append([wrows(wdown_d, f0, nf, jo * 128, 128)])
        ws3 = WStream(specs3)
        for blk in range(NB):
            t0 = blk * TB
            for ti in range(4):
                P.dma("sync", lambda h, ti=ti, t0=t0: h.dma_start(out=x2[:, ti, :], in_=x1s_d[t0 + ti * 128:t0 + (ti + 1) * 128, :]), writes=[Bx2[ti]])
                norm_tile_T(x2[:, ti, :], Bx2[ti], 128, lambda kc: a2[:, kc:kc + 1], lambda kc: ada(3, kc, 0), z2T, ti * 128, Bz2T)
            if dbg and blk == 0:
                dbg_out("z2T", z2T[:, 0, :], [128, TB], Bz2T, BF16)
            for j in range(NFC):
                wu, Bwu = ws3.next()
                wu2, Bwu2 = ws3.next()
                wuv = wu.rearrange("p (s k n) -> p s k n", s=2, k=8)
                wu2v = wu2.rearrange("p (s k n) -> p s k n", s=2, k=8)
                pu = []
                for s in range(2):
                    pt, Bp = next_pf((0, 1, 2, 3))
                    pu.append((pt, Bp))
                    for kc in range(16):
                        wsel, Bsel = (wuv, Bwu) if kc < 8 else (wu2v, Bwu2)
                        P.op("tensor", lambda h, pt=pt, wsel=wsel, s=s, kc=kc: h.matmul(
                            pt[:, :], lhsT=wsel[:, s, kc % 8, :], rhs=z2T[:, kc, :], start=(kc == 0), stop=(kc == 15)),
                            reads=[Bsel, Bz2T], writes=[Bp])
                    if blk == 0:
                        ph, Bph = pf[4 + s], Bpf[4 + s]
                        for kc in range(16):
                            wsel, Bsel = (wuv, Bwu) if kc < 8 else (wu2v, Bwu2)
                            P.op("tensor", lambda h, ph=ph, wsel=wsel, s=s, kc=kc: h.matmul(
                                ph[:, 0:6], lhsT=wsel[:, s, kc % 8, :], rhs=zh[:, kc, 0:6], start=(kc == 0), stop=(kc == 15)),
                                reads=[Bsel, Bzh], writes=[Bph])
                        P.op("vector", lambda h, ph=ph, s=s, j=j: h.tensor_copy(out=uh[:, s * NFC + j, 1:7], in_=ph[:, 0:6]), reads=[Bph], writes=[Buh])
                ws3.done()
                ci2 = j % 2
                for s in range(2):
                    pt, Bp = pu[s]
                    cdst, Bc = (ca[ci2], Bca[ci2]) if s == 0 else (cb[ci2], Bcb[ci2])
                    fj = s * NFC + j
                    P.op("scalar", lambda h, pt=pt, cdst=cdst, fj=fj: h.activation(out=cdst[:], in_=pt[:, :], func=AF.Identity,
                                                                                    scale=convp[:, fj, 1:2], bias=convp[:, fj, 3:4]),
                         reads=[Bp, Bconv], writes=[Bc])
                    P.op("vector", lambda h, pt=pt, cdst=cdst, fj=fj: h.scalar_tensor_tensor(
                        out=cdst[:, 1:TB], in0=pt[:, 0:TB - 1], scalar=convp[:, fj, 0:1], in1=cdst[:, 1:TB], op0=ALU.mult, op1=ALU.add),
                        reads=[Bp, Bconv, Bc], writes=[Bc])
                    P.op("vector", lambda h, pt=pt, cdst=cdst, fj=fj: h.scalar_tensor_tensor(
                        out=cdst[:, 0:TB - 1], in0=pt[:, 1:TB], scalar=convp[:, fj, 2:3], in1=cdst[:, 0:TB - 1], op0=ALU.mult, op1=ALU.add),
                        reads=[Bp, Bconv, Bc], writes=[Bc])
                    lcol = 2 * blk - 1 if blk > 0 else 0
                    rcol = 2 * blk + 2 if blk < NB - 1 else 7
                    P.op("vector", lambda h, cdst=cdst, fj=fj, lcol=lcol: h.scalar_tensor_tensor(
                        out=cdst[:, 0:1], in0=uh[:, fj, lcol:lcol + 1], scalar=convp[:, fj, 0:1], in1=cdst[:, 0:1], op0=ALU.mult, op1=ALU.add),
                        reads=[Buh, Bconv, Bc], writes=[Bc])
                    P.op("vector", lambda h, cdst=cdst, fj=fj, rcol=rcol: h.scalar_tensor_tensor(
                        out=cdst[:, TB - 1:TB], in0=uh[:, fj, rcol:rcol + 1], scalar=convp[:, fj, 2:3], in1=cdst[:, TB - 1:TB], op0=ALU.mult, op1=ALU.add),
                        reads=[Buh, Bconv, Bc], writes=[Bc])
                P.op("scalar", lambda h, ci2=ci2: h.activation(out=sa[ci2][:], in_=ca[ci2][:], func=AF.Silu), reads=[Bca[ci2]], writes=[Bsa[ci2]])
                P.op("gpsimd", lambda h, ci2=ci2, j=j: h.tensor_tensor(out=hT[:, j, :], in0=sa[ci2][:], in1=cb[ci2][:], op=ALU.mult),
                     reads=[Bsa[ci2], Bcb[ci2]], writes=[BhT])
            if dbg and blk == 0:
                dbg_out("hT", hT[:, 0, :], [128, TB], BhT, BF16)
            units = []
            for jo in range(16):
                def mm(jo=jo):
                    wparts = []
                    for (f0, nf) in ((0, 16), (16, 16), (32, 12)):
                        wparts.append((ws3.next(), f0, nf))
                    pt, Bp = next_pf((0, 1, 2, 3))
                    for (wd, Bwd), f0, nf in wparts:
                        wdv = wd.rearrange("p (k n) -> p k n", k=nf)
                        for f in range(nf):
                            P.op("tensor", lambda h, pt=pt, wdv=wdv, f=f, f0=f0: h.matmul(
                                pt[:, :], lhsT=wdv[:, f, :], rhs=hT[:, f0 + f, :], start=(f0 + f == 0), stop=(f0 + f == NFC - 1)),
                                reads=[Bwd, BhT], writes=[Bp])
                    ws3.done()
                    return pt, Bp

                def post(pt, Bp, jo=jo):
                    ai = jo % 2
                    P.op("scalar", lambda h, pt=pt, ai=ai, jo=jo: h.activation(out=aoT[ai][:], in_=pt[:, :], func=AF.Identity, scale=ada(5, jo, 0)),
                         reads=[Bp, Bada], writes=[BaoT[ai]])
                    p2, Bp2 = pf[4 + ai], Bpf[4 + ai]
                    for ti in range(4):
                        P.op("tensor", lambda h, p2=p2, ai=ai, ti=ti: h.transpose(out=p2[:, ti * 128:(ti + 1) * 128], in_=aoT[ai][:, ti * 128:(ti + 1) * 128], identity=identf[:]),
                             reads=[BaoT[ai], Bidf], writes=[Bp2])
                    P.op("vector", lambda h, p2=p2, jo=jo: h.tensor_tensor(
                        out=x2[:, :, jo * 128:(jo + 1) * 128], in0=p2[:, :].rearrange("p (t d) -> p t d", t=4), in1=x2[:, :, jo * 128:(jo + 1) * 128], op=ALU.add),
                        reads=[Bp2] + Bx2, writes=Bx2)
                units.append((mm, post))
            run_pipeline(units)
            for ti in range(4):
                stt, Bs = stat()
                P.op("scalar", lambda h, ti=ti, stt=stt: h.activation(out=junk[:, :], in_=x2[:, ti, :], func=AF.Square, accum_out=stt[:, 0:1]),
                     reads=[Bx2[ti]], writes=[Bjunk, Bs])
                rstd_from_ss(stt, Bs, 128, D)
                P.op("vector", lambda h, ti=ti, stt=stt: h.scalar_tensor_tensor(
                    out=x2[:, ti, :], in0=x2[:, ti, :], scalar=stt[:, 8:9], in1=gfin, op0=ALU.mult, op1=ALU.mult),
                    reads=[Bx2[ti], Bs, Bgfin], writes=[Bx2[ti]])
                P.dma("scalar", lambda h, ti=ti, t0=t0: h.dma_start(out=out_d[t0 + ti * 128:t0 + (ti + 1) * 128, :], in_=x2[:, ti, :]),
                      reads=[Bx2[ti]], sembuf=Bx2[ti])

        P.emit(final_bufs=list(Bx2) + dbg_bufs)
        print("nsem", P.nsem, "nwaits", P.nwaits, {e: len(P.recs[e]) for e in P.ENGS})
    return nc, dbg_outs


def _rope_tables(rot_dim, reps):
    n_rows = T // GRID_W
    row = np.repeat(np.arange(n_rows, dtype=np.float32), GRID_W)
    col = np.tile(np.arange(GRID_W, dtype=np.float32), n_rows)
    half = rot_dim // 2
    q = rot_dim // 4
    inv_freq = (np.float32(10000.0) ** (-np.arange(0, half, 2, dtype=np.float32) / np.float32(half))).astype(np.float32)
    ang = np.concatenate([row[:, None] * inv_freq, col[:, None] * inv_freq], axis=-1).astype(np.float32)
    c = np.cos(ang).astype(np.float32).reshape(T, 2, q)
    s = np.sin(ang).astype(np.float32).reshape(T, 2, q)
    C = np.stack([c, c], axis=2).reshape(T, rot_dim)
    S = np.stack([-s, s], axis=2).reshape(T, rot_dim)
    return np.ascontiguousarray(np.tile(C, (1, reps))), np.ascontiguousarray(np.tile(S, (1, reps)))


_CACHE = {}


def kernel(x, c, ctx, c_ctx, w_ada, b_ada, norm1_g, w_in, mla_q_norm_g, w_q_up, mla_kv_norm_g,
           w_kv_up, gqa_q_norm_g, gqa_k_norm_g, w_br_a, w_br_b, w_out, norm2_g, w_up, conv_w,
           conv_b, w_down, final_norm_g, _dbg=False):
    f = lambda a: np.ascontiguousarray(np.asarray(a, dtype=np.float32))
    x, c, ctx, c_ctx = f(x), f(c), f(ctx), f(c_ctx)
    import os
    stop = float(os.environ.get("K_STOP", "99")) if _dbg else 99
    ncores = int(os.environ.get("K_CORES", "8")) if _dbg else 8
    key = ("dbg" if _dbg else "prod", stop)
    if key not in _CACHE:
        _CACHE[key] = build_program(dbg=_dbg, stop=stop)
    nc, dbg_outs = _CACHE[key]
    AC, AS = _rope_tables(64, 8)
    BC, BS = _rope_tables(128, 1)

    def fm(v, n):
        return np.ascontiguousarray(f(v).reshape(n, 128).T)
    convp = np.stack([f(conv_w)[0, 0], f(conv_w)[0, 1], f(conv_w)[0, 2], f(conv_b)[0]], axis=-1)
    convp = np.ascontiguousarray(convp.reshape(88, 128, 4).transpose(1, 0, 2).reshape(128, 88 * 4))
    shared = {
        "w_ada": f(w_ada)[0], "badaT": fm(b_ada[0], 96), "n1g": fm(norm1_g[0], 16), "n2g": fm(norm2_g[0], 16),
        "w_in": f(w_in)[0], "gq": f(mla_q_norm_g)[0], "w_q_up": f(w_q_up)[0], "gkv": f(mla_kv_norm_g)[0],
        "w_kv_up": f(w_kv_up)[0], "gqq": f(gqa_q_norm_g)[0], "gqk": f(gqa_k_norm_g)[0],
        "w_br_a": f(w_br_a)[0], "w_br_b": f(w_br_b)[0], "w_out": f(w_out)[0], "w_up": f(w_up)[0],
        "convp": convp, "w_down": f(w_down)[0], "gfin": f(final_norm_g),
        "identf": np.eye(128, dtype=np.float32), "ropeAC": AC, "ropeAS": AS, "ropeBC": BC, "ropeBS": BS,
    }
    in_maps = []
    for b in range(ncores):
        cv = np.stack([c[b].reshape(16, 128).T, c_ctx.reshape(16, 128).T], axis=-1)
        m = dict(shared)
        m["x"] = x[b]
        m["ctx"] = ctx[b]
        m["cvec"] = np.ascontiguousarray(cv.reshape(128, 32))
        in_maps.append(m)
    res = run_bass_kernel_spmd(nc, in_maps, core_ids=list(range(ncores)))
    out = np.stack([np.asarray(r["out"], dtype=np.float32) for r in res.results], axis=0)
    if ncores < 8:
        out = np.concatenate([out, np.zeros((8 - ncores, T, D), np.float32)], 0)
    if _dbg:
        DEBUG.clear()
        for k in dbg_outs:
            DEBUG[k] = np.asarray(res.results[0]["dbg_" + k])
    return out
```
